# Optimizing a Trainium2 kernel written in Bass

```python
import math
import jax
import jax.numpy as jnp
from jax import lax
import numpy as np

D_MODEL = 1024
BATCH = 4
SEQ = 4096
DEPTH = 2

HEAD_DIM = 64
FOX_HEADS = 8
FOX_WIDTH = FOX_HEADS * HEAD_DIM
DIFF_HEADS = 4
DIFF_QK_WIDTH = DIFF_HEADS * 2 * HEAD_DIM
DIFF_V_DIM = 2 * HEAD_DIM
DIFF_WIDTH = DIFF_HEADS * DIFF_V_DIM
CONV_WIDTH = D_MODEL // 2
CONV_K = 3
N_BRANCH = 3
ROPE_THETA = 500000.0
ROT_DIM = HEAD_DIM // 4
Q_BLOCK = 128
RMS_EPS = 1e-6
MAX_POS_OFFSET = 1024
SPLIT_SIZES = (FOX_WIDTH, FOX_WIDTH, FOX_WIDTH, FOX_HEADS, FOX_WIDTH,
               DIFF_QK_WIDTH, DIFF_QK_WIDTH, DIFF_WIDTH, DIFF_WIDTH,
               CONV_WIDTH, CONV_WIDTH, CONV_WIDTH, CONV_WIDTH,
               N_BRANCH * D_MODEL)
IN_WIDTH = (4 * FOX_WIDTH + FOX_HEADS + 2 * DIFF_QK_WIDTH + 2 * DIFF_WIDTH
            + 4 * CONV_WIDTH + N_BRANCH * D_MODEL)

kernel_name = 'hybrid_fox_diffattn_shortconv_gated_merge'


def _rmsnorm(x, g):
    xf = x.astype(jnp.float32)
    y = xf * lax.rsqrt(jnp.mean(xf * xf, axis=-1, keepdims=True) + RMS_EPS)
    return (y * g.astype(jnp.float32)).astype(x.dtype)


def _split_cols(p):
    idx, acc = [], 0
    for s in SPLIT_SIZES[:-1]:
        acc += s
        idx.append(acc)
    return jnp.split(p, idx, axis=-1)


def _rope_tables(positions):
    inv_freq = ROPE_THETA ** (-jnp.arange(0, ROT_DIM, 2, dtype=jnp.float32) / ROT_DIM)
    ang = positions.astype(jnp.float32)[..., None] * inv_freq
    return jnp.cos(ang), jnp.sin(ang)


def _partial_rotary(x, cos, sin):
    half = ROT_DIM // 2
    x1 = x[..., :half].astype(jnp.float32)
    x2 = x[..., half:ROT_DIM].astype(jnp.float32)
    r1 = (x1 * cos - x2 * sin).astype(x.dtype)
    r2 = (x2 * cos + x1 * sin).astype(x.dtype)
    return jnp.concatenate([r1, r2, x[..., ROT_DIM:]], axis=-1)


def _causal_mask(s0, s1):
    q_idx = jnp.arange(s0, s1)[:, None]
    k_idx = jnp.arange(s1)[None, :]
    return k_idx <= q_idx


def _fox_attention(q, k, v, log_f):
    b, s, h, d = q.shape
    q = q.transpose(0, 2, 1, 3)
    k = k.transpose(0, 2, 1, 3)
    v = v.transpose(0, 2, 1, 3)
    c = jnp.cumsum(log_f, axis=1).transpose(0, 2, 1)
    scale = HEAD_DIM ** -0.5
    outs = []
    for s0 in range(0, s, Q_BLOCK):
        s1 = s0 + Q_BLOCK
        logits = jnp.einsum('bhqd,bhkd->bhqk', q[:, :, s0:s1], k[:, :, :s1]).astype(jnp.float32) * scale
        logits = logits + c[:, :, s0:s1, None] - c[:, :, None, :s1]
        p = jax.nn.softmax(jnp.where(_causal_mask(s0, s1), logits, -jnp.inf), axis=-1)
        outs.append(jnp.einsum('bhqk,bhkd->bhqd', p.astype(v.dtype), v[:, :, :s1]))
    o = jnp.concatenate(outs, axis=2)
    return o.transpose(0, 2, 1, 3).reshape(b, s, h * d)


def _diff_attention(q, k, v, lam, lam_init, norm_g):
    b, s, h, _, d = q.shape
    q = q.transpose(0, 2, 3, 1, 4)
    k = k.transpose(0, 2, 3, 1, 4)
    v = v.transpose(0, 2, 1, 3)
    scale = HEAD_DIM ** -0.5
    outs = []
    for s0 in range(0, s, Q_BLOCK):
        s1 = s0 + Q_BLOCK
        logits = jnp.einsum('bhcqd,bhckd->bhcqk', q[:, :, :, s0:s1], k[:, :, :, :s1]).astype(jnp.float32) * scale
        p = jax.nn.softmax(jnp.where(_causal_mask(s0, s1), logits, -jnp.inf), axis=-1)
        a = p[:, :, 0] - lam * p[:, :, 1]
        outs.append(jnp.einsum('bhqk,bhkv->bhqv', a.astype(v.dtype), v[:, :, :s1]))
    o = jnp.concatenate(outs, axis=2)
    o = _rmsnorm(o, norm_g) * (1.0 - lam_init)
    return o.transpose(0, 2, 1, 3).reshape(b, s, h * DIFF_V_DIM)


def _short_conv(u, w):
    rhs = w.astype(u.dtype)[:, None, :]
    return lax.conv_general_dilated(
        u, rhs, window_strides=(1,), padding=[(CONV_K - 1, 0)],
        dimension_numbers=('NWC', 'WIO', 'NWC'), feature_group_count=u.shape[-1])


def _layer(x, cos, sin, layer, pre_g, w_in, b_f, b_m, conv_w, lq1, lk1, lq2, lk2,
           diff_g, w_fox, w_diff, w_conv, w_o, post_g):
    b, s, _ = x.shape
    h = _rmsnorm(x, pre_g)
    p = h @ w_in
    (fq, fk, fv, ff, fg, dq, dk, dv, dg, cb, cc, cx, cg, mg) = _split_cols(p)

    log_f = jax.nn.log_sigmoid((ff + b_f).astype(jnp.float32))
    y_fox = _fox_attention(fq.reshape(b, s, FOX_HEADS, HEAD_DIM),
                           fk.reshape(b, s, FOX_HEADS, HEAD_DIM),
                           fv.reshape(b, s, FOX_HEADS, HEAD_DIM), log_f)
    y_fox = y_fox * jax.nn.silu(fg)

    lam_init = 0.8 - 0.6 * math.exp(-0.3 * layer)
    f32 = jnp.float32
    lam = (jnp.exp(jnp.sum(lq1.astype(f32) * lk1.astype(f32)))
           - jnp.exp(jnp.sum(lq2.astype(f32) * lk2.astype(f32))) + lam_init)
    cs, sn = cos[:, :, None, None, :], sin[:, :, None, None, :]
    q_d = _partial_rotary(dq.reshape(b, s, DIFF_HEADS, 2, HEAD_DIM), cs, sn)
    k_d = _partial_rotary(dk.reshape(b, s, DIFF_HEADS, 2, HEAD_DIM), cs, sn)
    y_diff = _diff_attention(q_d, k_d, dv.reshape(b, s, DIFF_HEADS, DIFF_V_DIM), lam, lam_init, diff_g)
    y_diff = y_diff * jax.nn.silu(dg)

    y_conv = cb * _short_conv(cc * cx, conv_w)
    y_conv = y_conv * jax.nn.silu(cg)

    gates = jax.nn.sigmoid(mg + b_m).reshape(b, s, N_BRANCH, D_MODEL)
    m = (gates[:, :, 0] * (y_fox @ w_fox)
         + gates[:, :, 1] * (y_diff @ w_diff)
         + gates[:, :, 2] * (y_conv @ w_conv))
    o = m @ w_o
    return x + _rmsnorm(o, post_g)


def setup_inputs(seed: int = 0) -> dict:
    key = jax.random.key(seed)
    ks = jax.random.split(key, 20)
    f32 = jnp.float32

    def nrm(k, shape, scale):
        return jax.random.normal(k, shape, f32) * scale

    x = jax.random.normal(ks[0], (BATCH, SEQ, D_MODEL), f32)
    offset = jax.random.randint(ks[1], (BATCH, 1), 0, MAX_POS_OFFSET, dtype=jnp.int32)
    positions = offset + jnp.arange(SEQ, dtype=jnp.int32)[None, :]
    pre_norm_g = 1.0 + nrm(ks[2], (DEPTH, D_MODEL), 0.02)
    w_in = nrm(ks[3], (DEPTH, D_MODEL, IN_WIDTH), D_MODEL ** -0.5)
    b_forget = jax.random.uniform(ks[4], (DEPTH, FOX_HEADS), f32, 1.0, 6.0)
    b_merge = nrm(ks[5], (DEPTH, N_BRANCH * D_MODEL), 0.01)
    conv_w = nrm(ks[6], (DEPTH, CONV_K, CONV_WIDTH), CONV_K ** -0.5)
    lam_q1 = nrm(ks[7], (DEPTH, HEAD_DIM), 0.1)
    lam_k1 = nrm(ks[8], (DEPTH, HEAD_DIM), 0.1)
    lam_q2 = nrm(ks[9], (DEPTH, HEAD_DIM), 0.1)
    lam_k2 = nrm(ks[10], (DEPTH, HEAD_DIM), 0.1)
    diff_norm_g = 1.0 + nrm(ks[11], (DEPTH, DIFF_V_DIM), 0.02)
    w_br_fox = nrm(ks[12], (DEPTH, FOX_WIDTH, D_MODEL), FOX_WIDTH ** -0.5)
    w_br_diff = nrm(ks[13], (DEPTH, DIFF_WIDTH, D_MODEL), DIFF_WIDTH ** -0.5)
    w_br_conv = nrm(ks[14], (DEPTH, CONV_WIDTH, D_MODEL), CONV_WIDTH ** -0.5)
    w_out = nrm(ks[15], (DEPTH, D_MODEL, D_MODEL), D_MODEL ** -0.5)
    post_norm_g = 1.0 + nrm(ks[16], (DEPTH, D_MODEL), 0.02)
    return {'x': x, 'positions': positions, 'pre_norm_g': pre_norm_g, 'w_in': w_in,
            'b_forget': b_forget, 'b_merge': b_merge, 'conv_w': conv_w,
            'lam_q1': lam_q1, 'lam_k1': lam_k1, 'lam_q2': lam_q2, 'lam_k2': lam_k2,
            'diff_norm_g': diff_norm_g, 'w_br_fox': w_br_fox, 'w_br_diff': w_br_diff,
            'w_br_conv': w_br_conv, 'w_out': w_out, 'post_norm_g': post_norm_g}


def reference(x, positions, pre_norm_g, w_in, b_forget, b_merge, conv_w, lam_q1, lam_k1,
              lam_q2, lam_k2, diff_norm_g, w_br_fox, w_br_diff, w_br_conv, w_out, post_norm_g):
    cos, sin = _rope_tables(positions)
    for l in range(DEPTH):
        x = _layer(x, cos, sin, l, pre_norm_g[l], w_in[l], b_forget[l], b_merge[l], conv_w[l],
                   lam_q1[l], lam_k1[l], lam_q2[l], lam_k2[l], diff_norm_g[l],
                   w_br_fox[l], w_br_diff[l], w_br_conv[l], w_out[l], post_norm_g[l])
    return x
```

```python
import math
import numpy as np
from contextlib import ExitStack
import concourse.bass as bass
import concourse.mybir as mybir
from concourse.bass_utils import run_bass_kernel_spmd

F32 = mybir.dt.float32; BF16 = mybir.dt.bfloat16; I32 = mybir.dt.int32; U8 = mybir.dt.uint8
AF = mybir.ActivationFunctionType; ALU = mybir.AluOpType; AX = mybir.AxisListType

D = 1024; SEQ = 4096; BATCH = 4; DEPTH = 2; NCORES = 8
NT = SEQ // 128; NB = SEQ // 512
EPS = 1e-6
NBLK = 19
FFCOL = NBLK * 512
WCOLS = FFCOL + 8
NPAR = 8 + 24 + 12 + 1 + 1 + 256 + 2
P_G, P_BM, P_CW, P_DG, P_BF, P_LAM, P_ROPE = 0, 8, 32, 44, 45, 46, 302

ENGS = ('pe', 'dve', 'act', 'pool', 'sp')
CAP = 2000; NSLOT = 8; DCAP = 100


class Op:
    __slots__ = ('eng', 'fn', 'deps', 'sig', 'sem', 'val', 'dma', 'nobar')


class Sched:
    def __init__(self):
        self.ops = {e: [] for e in ENGS}
        self.lastw = {}
        self.readers = {}
        self.pending_dma = []

    def op(self, eng, fn, reads=(), writes=(), dma=False, nobar=False, extra=()):
        o = Op(); o.eng = eng; o.fn = fn; o.dma = dma; o.deps = []; o.sig = dma
        o.sem = None; o.val = None; o.nobar = nobar
        deps = set(extra)
        for r in reads:
            w = self.lastw.get(r)
            if w is not None: deps.add(w)
        for r in writes:
            w = self.lastw.get(r)
            if w is not None: deps.add(w)
            rd = self.readers.get(r)
            if rd:
                deps.update(rd[0].values()); deps.update(rd[1])
        for r in writes:
            self.lastw[r] = o; self.readers[r] = ({}, [])
        for r in reads:
            if r in writes: continue
            rd = self.readers.setdefault(r, ({}, []))
            if dma: rd[1].append(o)
            else: rd[0][eng] = o
        for d in deps:
            if d is o: continue
            if d.eng == 'pe' and eng == 'pe' and not d.dma and not dma: continue
            o.deps.append(d)
        self.ops[eng].append(o)
        if dma and not nobar: self.pending_dma.append(o)
        return o

    def barrier(self):
        last = [self.ops[e][-1] for e in ENGS if self.ops[e] and not self.ops[e][-1].dma]
        lastc = []
        for e in ENGS:
            for o in reversed(self.ops[e]):
                if not o.dma and o.fn is not None:
                    lastc.append(o); break
        ex = lastc + self.pending_dma
        self.pending_dma = []
        for e in ENGS:
            self.op(e, None, extra=ex)

    def emit(self, nc):
        for e in ENGS:
            for o in self.ops[e]:
                for d in o.deps: d.sig = True
        with ExitStack() as st:
            semcache = {}

            def getsem(key):
                if key not in semcache:
                    semcache[key] = st.enter_context(nc.semaphore("s_%s" % "_".join(map(str, key))))
                return semcache[key]
            for e in ENGS:
                k = 0; j = 0
                for o in self.ops[e]:
                    if o.fn is None: continue
                    if o.dma:
                        slot = j % NSLOT; n = j // NSLOT
                        o.sem = getsem(('d', e, slot, n // DCAP)); o.val = 16 * (n % DCAP + 1); j += 1
                    elif o.sig:
                        o.sem = getsem(('c', e, k // CAP)); o.val = k % CAP + 1; k += 1
            block = st.enter_context(nc.Block())

            def run(e):
                def body(eng):
                    seen = {}
                    for o in self.ops[e]:
                        need = {}
                        for d in o.deps:
                            if d.sem is None: continue
                            key = id(d.sem)
                            if need.get(key, (None, 0))[1] < d.val: need[key] = (d.sem, d.val)
                        for key, (sem, val) in need.items():
                            if seen.get(key, 0) < val:
                                eng.wait_ge(sem, val); seen[key] = val
                        if o.fn is None: continue
                        ins = o.fn(eng)
                        if o.dma: ins.then_inc(o.sem, 16)
                        elif o.sig: ins.then_inc(o.sem, 1)
                return body
            block.tensor(run('pe')); block.vector(run('dve')); block.scalar(run('act'))
            block.gpsimd(run('pool')); block.sync(run('sp'))


def I(name, *a, **kw):
    return lambda e: getattr(e, name)(*a, **kw)


_DTB = {F32: 4, BF16: 2, I32: 4}


def build_program(debug=False, stop=None):
    nc = bass.Bass("TRN2", target_bir_lowering=False)
    S = Sched()
    x_in = nc.dram_tensor("x", [SEQ, D], F32, kind="ExternalInput").ap()
    pos_in = nc.dram_tensor("pos", [1, SEQ], I32, kind="ExternalInput").ap()
    win = nc.dram_tensor("win", [DEPTH, D, WCOLS], F32, kind="ExternalInput").ap()
    wbr_in = nc.dram_tensor("wbr", [DEPTH, 1536, D], F32, kind="ExternalInput").ap()
    wo_in = nc.dram_tensor("wo", [DEPTH, D, D], F32, kind="ExternalInput").ap()
    par_in = nc.dram_tensor("par", [DEPTH, 128, NPAR], F32, kind="ExternalInput").ap()
    pg_in = nc.dram_tensor("postg", [DEPTH, 1, D], F32, kind="ExternalInput").ap()
    out = nc.dram_tensor("out", [SEQ, D], F32, kind="ExternalOutput").ap()
    sk = "ExternalOutput" if debug else "Internal"
    xs = nc.dram_tensor("xs", [SEQ, D], F32, kind=sk).ap()
    cs_d = nc.dram_tensor("cs_d", [2, 8, 3, SEQ], BF16, kind=sk).ap()
    rot_d = nc.dram_tensor("rot_d", [2, 128, SEQ], BF16, kind=sk).ap()
    rope_d = nc.dram_tensor("rope_d", [2, 128, SEQ], F32, kind=sk).ap()
    yT_d = nc.dram_tensor("yT_d", [12, 128, SEQ], BF16, kind=sk).ap()
    hT_d = nc.dram_tensor("hT_d", [128, 8, SEQ], BF16, kind=sk).ap() if debug else None

    def stop_here(name):
        if stop == name:
            if debug:
                S.op('sp', I('dma_start', out=hT_d, in_=hT[:, :, :]), reads=['hT'], writes=['hT_d'], dma=True)
            S.barrier(); S.emit(nc)
            return True
        return False

    TOTAL = 206 * 1024
    arena = nc.alloc_sbuf_tensor("arena", [128, TOTAL], U8)
    pb = [nc.alloc_psum_tensor("pb%d" % i, [128, 512], F32) for i in range(8)]
    pbn = ['pb%d' % i for i in range(8)]

    class Region:
        def __init__(self, base, size): self.base = base; self.size = size; self.off = 0
        def reset(self): self.off = 0
        def alloc(self, shape, dt):
            n = int(np.prod(shape[1:])) * _DTB[dt]
            n = (n + 31) // 32 * 32
            assert self.off + n <= self.size, ("region overflow", self.off, n, self.size)
            ap = arena[0:shape[0], self.base + self.off: self.base + self.off + int(np.prod(shape[1:])) * _DTB[dt]].bitcast(dt)
            if len(shape) == 3:
                ap = ap.rearrange("p (a b) -> p a b", a=shape[1])
            elif len(shape) == 4:
                ap = ap.rearrange("p (a b c) -> p a b c", a=shape[1], b=shape[2])
            self.off += n
            return ap

    CONST = Region(0, 6 * 1024)
    HT = Region(6 * 1024, 64 * 1024)
    WORK = Region(70 * 1024, TOTAL - 70 * 1024)

    ident = CONST.alloc([128, 128], BF16)
    ones_bf = CONST.alloc([128, 128], BF16)
    tri = CONST.alloc([128, 128], BF16)
    ones_f = CONST.alloc([128, 128], F32)
    par = [CONST.alloc([128, NPAR], F32) for _ in range(DEPTH)]
    small = CONST.alloc([128, 16], F32)
    hT = HT.alloc([128, 8, SEQ], BF16)

    S.op('pool', I('memset', ident[:, :], 1.0), writes=['ident'])
    S.op('pool', I('affine_select', out=ident[:, :], in_=ident[:, :], pattern=[[-1, 128]], compare_op=ALU.is_equal,
                   fill=0.0, base=0, channel_multiplier=1), reads=['ident'], writes=['ident'])
    S.op('pool', I('memset', ones_bf[:, :], 1.0), writes=['ones_bf'])
    S.op('pool', I('memset', ones_f[:, :], 1.0), writes=['ones_f'])
    S.op('pool', I('memset', tri[:, :], 1.0), writes=['tri'])
    S.op('pool', I('affine_select', out=tri[:, :], in_=tri[:, :], pattern=[[1, 128]], compare_op=ALU.is_ge,
                   fill=0.0, base=0, channel_multiplier=-1), reads=['tri'], writes=['tri'])
    for l in range(DEPTH):
        S.op('sp', I('dma_start', out=par[l][:, :], in_=par_in[l]), writes=['par%d' % l], dma=True)

    WORK.reset()
    posi = WORK.alloc([128, SEQ], I32)
    ang = WORK.alloc([128, SEQ], F32)
    tmp = WORK.alloc([128, SEQ], F32)
    r2 = WORK.alloc([128, SEQ], F32)
    TWO_PI = float(2 * np.pi); PI = float(np.pi)
    S.op('sp', I('dma_start', out=posi[:, :], in_=pos_in.partition_broadcast(128)), writes=['posi'], dma=True)
    S.op('dve', I('tensor_copy', out=ang[:, :], in_=posi[:, :]), reads=['posi'], writes=['ang'])
    S.op('dve', I('tensor_scalar', out=ang[:, :], in0=ang[:, :], scalar1=par[0][:, P_ROPE:P_ROPE + 1], scalar2=None, op0=ALU.mult),
         reads=['ang', 'par0'], writes=['ang'])
    S.op('dve', I('tensor_scalar', out=tmp[:, :], in0=ang[:, :], scalar1=float(1.0 / TWO_PI), scalar2=None, op0=ALU.mult),
         reads=['ang'], writes=['tmp'])
    S.op('dve', I('tensor_copy', out=posi[:, :], in_=tmp[:, :]), reads=['tmp'], writes=['posi'])
    S.op('dve', I('tensor_copy', out=tmp[:, :], in_=posi[:, :]), reads=['posi'], writes=['tmp'])
    S.op('dve', I('scalar_tensor_tensor', out=ang[:, :], in0=tmp[:, :], scalar=-TWO_PI, in1=ang[:, :], op0=ALU.mult, op1=ALU.add),
         reads=['tmp', 'ang'], writes=['ang'])
    S.op('dve', I('tensor_scalar', out=ang[:, :], in0=ang[:, :], scalar1=PI, scalar2=-PI, op0=ALU.min, op1=ALU.max),
         reads=['ang'], writes=['ang'])
    S.op('dve', I('tensor_scalar', out=r2[:, :], in0=ang[:, :], scalar1=float(PI / 2), scalar2=None, op0=ALU.add), reads=['ang'], writes=['r2'])
    S.op('dve', I('tensor_single_scalar', out=tmp[:, :], in_=r2[:, :], scalar=PI, op=ALU.is_gt), reads=['r2'], writes=['tmp'])
    S.op('dve', I('scalar_tensor_tensor', out=r2[:, :], in0=tmp[:, :], scalar=-TWO_PI, in1=r2[:, :], op0=ALU.mult, op1=ALU.add),
         reads=['tmp', 'r2'], writes=['r2'])
    S.op('dve', I('tensor_scalar', out=r2[:, :], in0=r2[:, :], scalar1=PI, scalar2=-PI, op0=ALU.min, op1=ALU.max), reads=['r2'], writes=['r2'])
    S.op('act', I('activation', out=tmp[:, :], in_=r2[:, :], func=AF.Sin), reads=['r2'], writes=['tmp'])
    S.op('sp', I('dma_start', out=rope_d[0], in_=tmp[:, :]), reads=['tmp'], writes=['rope_d'], dma=True)
    S.op('act', I('activation', out=r2[:, :], in_=ang[:, :], func=AF.Sin, scale=par[0][:, P_ROPE + 1:P_ROPE + 2]), reads=['ang', 'par0', 'r2'], writes=['r2'])
    S.op('sp', I('dma_start', out=rope_d[1], in_=r2[:, :]), reads=['r2'], writes=['rope_d'], dma=True)
    S.barrier()
    if stop_here('rope'): return nc

    wcount = [0]

    def load_wblock(wbuf, wname, l, col0, ncols=512):
        S.op('pool', I('dma_start', out=wbuf[:, :, 0:ncols], in_=win[l][:, col0:col0 + ncols].rearrange("(c p) n -> p c n", p=128)),
             writes=[wname], dma=True)

    def proj_fm(wbuf, wname, c0, M, nb, bank):
        for kc in range(8):
            S.op('pe', I('matmul', pb[bank][0:M, :], lhsT=wbuf[:, kc, c0:c0 + M], rhs=hT[:, kc, nb * 512:(nb + 1) * 512],
                         start=(kc == 0), stop=(kc == 7)), reads=[wname, 'hT'], writes=[pbn[bank]])

    def proj_tm(wbuf, wname, c0, dst_fn, dname):
        for g in range(NT // 4):
            bank = g % 4
            for i in range(4):
                tt = g * 4 + i
                for kc in range(8):
                    S.op('pe', I('matmul', pb[bank][:, i * 128:(i + 1) * 128], lhsT=hT[:, kc, tt * 128:(tt + 1) * 128],
                                 rhs=wbuf[:, kc, c0:c0 + 128], start=(kc == 0), stop=(kc == 7)), reads=[wname, 'hT'], writes=[pbn[bank]])
            S.op('dve', dst_fn(g, pb[bank]), reads=[pbn[bank]], writes=[dname])

    for l in range(DEPTH):
        lam_init = 0.8 - 0.6 * math.exp(-0.3 * l)
        P = par[l]; pn = 'par%d' % l
        xsrc = x_in if l == 0 else xs
        xdst = xs if l < DEPTH - 1 else out

        WORK.reset()
        xt = [WORK.alloc([128, D], F32) for _ in range(2)]
        xn = [WORK.alloc([128, D], BF16) for _ in range(2)]
        junk = WORK.alloc([128, D], BF16)
        ss = WORK.alloc([128, NT], F32)
        rs = WORK.alloc([128, NT], F32)
        S.op('dve', I('memset', ss[:, :], 0.0), writes=['ss'])
        for t in range(NT):
            b = t % 2
            S.op('sp', I('dma_start', out=xt[b][:, :], in_=xsrc[t * 128:(t + 1) * 128, :]), writes=['xt%d' % b], dma=True)
            S.op('act', I('activation', out=junk[:, :], in_=xt[b][:, :], func=AF.Square, accum_out=ss[:, t:t + 1]),
                 reads=['xt%d' % b, 'ss'], writes=['junk', 'ssc%d' % t])
            S.op('act', I('activation', out=rs[:, t:t + 1], in_=ss[:, t:t + 1], func=AF.Sqrt, bias=EPS, scale=1.0 / D),
                 reads=['ssc%d' % t], writes=['rs%d' % t])
            S.op('dve', I('reciprocal', out=rs[:, t:t + 1], in_=rs[:, t:t + 1]), reads=['rs%d' % t], writes=['rs%d' % t])
            S.op('act', I('mul', out=xn[b][:, :], in_=xt[b][:, :], mul=rs[:, t:t + 1]), reads=['xt%d' % b, 'rs%d' % t], writes=['xn%d' % b])
            pbt = pb[b][:, :].bitcast(BF16)
            for c in range(8):
                S.op('pe', I('transpose', pbt[:, c * 128:(c + 1) * 128], xn[b][:, c * 128:(c + 1) * 128], ident[:, :]),
                     reads=['xn%d' % b, 'ident'], writes=[pbn[b]])
            S.op('dve', I('tensor_tensor', out=hT[:, :, t * 128:(t + 1) * 128], in0=pbt.rearrange("p (c n) -> p c n", c=8),
                          in1=P[:, P_G:P_G + 8].unsqueeze(2).to_broadcast([128, 8, 128]), op=ALU.mult),
                 reads=[pbn[b], pn], writes=['hT'])
        S.barrier()
        if l == 0 and stop_here('A'): return nc

        WORK.reset()
        lt = WORK.alloc([128, 128], F32)
        S.op('dve', I('tensor_tensor', out=lt[:, 0:64], in0=P[:, P_LAM:P_LAM + 64], in1=P[:, P_LAM + 64:P_LAM + 128], op=ALU.mult), reads=[pn], writes=['lt'])
        S.op('dve', I('tensor_tensor', out=lt[:, 64:128], in0=P[:, P_LAM + 128:P_LAM + 192], in1=P[:, P_LAM + 192:P_LAM + 256], op=ALU.mult), reads=[pn, 'lt'], writes=['lt'])
        S.op('dve', I('tensor_reduce', out=small[:, 3:5], in_=lt[:, :].rearrange("p (a b) -> p a b", a=2), axis=AX.X, op=ALU.add), reads=['lt'], writes=['sm34'])
        S.op('act', I('activation', out=small[:, 5:7], in_=small[:, 3:5], func=AF.Exp), reads=['sm34'], writes=['sm56'])
        S.op('dve', I('tensor_tensor', out=small[:, 0:1], in0=small[:, 6:7], in1=small[:, 5:6], op=ALU.subtract), reads=['sm56'], writes=['neglam'])
        S.op('dve', I('tensor_scalar', out=small[:, 0:1], in0=small[:, 0:1], scalar1=float(-lam_init), scalar2=None, op0=ALU.add), reads=['neglam'], writes=['neglam'])
        S.op('dve', I('tensor_scalar', out=small[:, 1:2], in0=P[:, P_BF:P_BF + 1], scalar1=-1.0, scalar2=None, op0=ALU.mult), reads=[pn], writes=['negb'])
        S.op('dve', I('tensor_scalar', out=small[:, 2:3], in0=P[:, P_DG:P_DG + 1], scalar1=float(1.0 - lam_init), scalar2=None, op0=ALU.mult), reads=[pn], writes=['gprime'])

        wff = WORK.alloc([128, 8, 8], BF16)
        et = [WORK.alloc([8, 512], F32) for _ in range(2)]
        Lraw = WORK.alloc([8, SEQ], F32)
        Lc = WORK.alloc([8, SEQ], F32)
        csb = WORK.alloc([8, 3, SEQ], BF16)
        csn = WORK.alloc([8, 3, SEQ], BF16)
        S.op('pool', I('dma_start', out=wff[:, :, :], in_=win[l][:, FFCOL:FFCOL + 8].rearrange("(c p) n -> p c n", p=128)), writes=['wff'], dma=True)
        for nb in range(NB):
            b = nb % 2
            for kc in range(8):
                S.op('pe', I('matmul', pb[b][0:8, :], lhsT=wff[:, kc, 0:8], rhs=hT[:, kc, nb * 512:(nb + 1) * 512], start=(kc == 0), stop=(kc == 7)),
                     reads=['wff', 'hT'], writes=[pbn[b]])
            S.op('act', I('activation', out=et[b][:, :], in_=pb[b][0:8, :], func=AF.Exp, bias=small[0:8, 1:2], scale=-1.0),
                 reads=[pbn[b], 'negb'], writes=['et%d' % b])
            S.op('act', I('activation', out=Lraw[:, nb * 512:(nb + 1) * 512], in_=et[b][:, :], func=AF.Ln, bias=1.0, scale=1.0),
                 reads=['et%d' % b], writes=['Lraw'])
        S.op('dve', I('tensor_tensor_scan', out=Lc[:, :], data0=ones_f[0:8, 0:1].to_broadcast([8, SEQ]), data1=Lraw[:, :], initial=0.0,
                      op0=ALU.mult, op1=ALU.add), reads=['Lraw', 'ones_f'], writes=['Lc'])
        S.op('dve', I('tensor_copy', out=csb[:, 0, :], in_=Lc[:, :]), reads=['Lc'], writes=['csb'])
        S.op('dve', I('tensor_tensor', out=Lraw[:, :], in0=Lc[:, :], in1=csb[:, 0, :], op=ALU.subtract), reads=['Lc', 'csb'], writes=['Lraw'])
        S.op('dve', I('tensor_copy', out=csb[:, 1, :], in_=Lraw[:, :]), reads=['Lraw'], writes=['csb'])
        S.op('dve', I('tensor_tensor', out=Lc[:, :], in0=Lraw[:, :], in1=csb[:, 1, :], op=ALU.subtract), reads=['Lraw', 'csb'], writes=['Lc'])
        S.op('dve', I('tensor_copy', out=csb[:, 2, :], in_=Lc[:, :]), reads=['Lc'], writes=['csb'])
        S.op('dve', I('tensor_scalar', out=csn[:, :, :], in0=csb[:, :, :], scalar1=-1.0, scalar2=None, op0=ALU.mult), reads=['csb'], writes=['csn'])
        S.op('sp', I('dma_start', out=cs_d[0], in_=csb[:, :, :]), reads=['csb'], writes=['cs_d'], dma=True)
        S.op('sp', I('dma_start', out=cs_d[1], in_=csn[:, :, :]), reads=['csn'], writes=['cs_d'], dma=True)
        S.barrier()
        if l == 0 and stop_here('B0'): return nc

        WORK.reset()
        wb = [WORK.alloc([128, 8, 512], BF16) for _ in range(2)]
        QK = [WORK.alloc([128, SEQ], BF16) for _ in range(4)]
        Vaug = WORK.alloc([128, NT, 2, 128], BF16)
        Gb = WORK.alloc([128, SEQ], BF16)
        ybuf = WORK.alloc([128, SEQ], BF16)
        Et = [WORK.alloc([128, 512], BF16) for _ in range(4)]
        rsb = WORK.alloc([128, 512], F32)
        tb = WORK.alloc([128, 512], F32)
        for i in range(4):
            S.op('pool', I('memset', QK[i][64:70, :], 1.0), writes=['QK%d' % i])
        S.op('pool', I('memset', Vaug[:, :, :, 64:128], 1.0), writes=['Vaug'])
        load_wblock(wb[0], 'wb0', l, 0)
        ecount = [0]
        for j in range(4):
            w = wb[j % 2]; wn = 'wb%d' % (j % 2)
            if j + 1 < 4:
                load_wblock(wb[(j + 1) % 2], 'wb%d' % ((j + 1) % 2), l, (j + 1) * 512)
            for nb in range(NB):
                sl = slice(nb * 512, (nb + 1) * 512)
                bq = (2 * nb) % 4; bk = (2 * nb + 1) % 4
                proj_fm(w, wn, 0, 128, nb, bq)
                S.op('act', I('mul', out=QK[0][0:64, sl], in_=pb[bq][0:64, :], mul=0.125), reads=[pbn[bq]], writes=['QK0'])
                S.op('act', I('mul', out=QK[1][0:64, sl], in_=pb[bq][64:128, :], mul=0.125), reads=[pbn[bq]], writes=['QK1'])
                proj_fm(w, wn, 128, 128, nb, bk)
                S.op('dve', I('tensor_copy', out=QK[2][0:64, sl], in_=pb[bk][0:64, :]), reads=[pbn[bk]], writes=['QK2'])
                S.op('dve', I('tensor_copy', out=QK[3][0:64, sl], in_=pb[bk][64:128, :]), reads=[pbn[bk]], writes=['QK3'])
            for hh in range(2):
                head = 2 * j + hh
                S.op('sp', I('dma_start', out=QK[hh][64:67, :], in_=cs_d[1, head]), reads=['cs_d'], writes=['QK%d' % hh], dma=True)
                S.op('sp', I('dma_start', out=QK[2 + hh][67:70, :], in_=cs_d[0, head]), reads=['cs_d'], writes=['QK%d' % (2 + hh)], dma=True)
            for nb in range(NB):
                sl = slice(nb * 512, (nb + 1) * 512)
                bg = nb % 4
                proj_fm(w, wn, 384, 128, nb, bg)
                S.op('act', I('activation', out=Gb[:, sl], in_=pb[bg][:, :], func=AF.Silu), reads=[pbn[bg]], writes=['Gb'])
            proj_tm(w, wn, 256,
                    lambda g, bank: I('tensor_copy', out=Vaug[:, g * 4:(g + 1) * 4, :, 0:64],
                                      in_=bank[:, :].rearrange("p (t h d) -> p t h d", t=4, h=2)), 'Vaug')
            for hh in range(2):
                Qn, Kn = 'QK%d' % hh, 'QK%d' % (2 + hh)
                Qa, Ka = QK[hh], QK[2 + hh]
                r0 = 64 * hh
                T = [(qb, kt) for qb in range(NB) for kt in range(4 * (qb + 1))]
                info = {}
                SBK = [3, 4, 5]; LOOK = 2

                def emit_S(i):
                    qb, kt = T[i]
                    jd = kt - 4 * qb
                    c0 = 128 * jd if jd > 0 else 0
                    sb = SBK[ecount[0] % 3]; E = Et[ecount[0] % 4]; En = 'E%d' % (ecount[0] % 4); ecount[0] += 1
                    info[i] = (sb, E, En, c0, jd)
                    S.op('pe', I('matmul', pb[sb][:, c0:512], lhsT=Ka[0:70, kt * 128:(kt + 1) * 128], rhs=Qa[0:70, qb * 512 + c0:(qb + 1) * 512],
                                 start=True, stop=True), reads=[Qn, Kn], writes=[pbn[sb]])

                def emit_rest(i):
                    qb, kt = T[i]
                    sb, E, En, c0, jd = info.pop(i)
                    nkt = 4 * (qb + 1); ob = 6 + (qb % 2)
                    S.op('act', I('activation', out=E[:, c0:512], in_=pb[sb][:, c0:512], func=AF.Exp), reads=[pbn[sb]], writes=[En])
                    if jd >= 0:
                        S.op('dve', I('tensor_tensor', out=E[:, c0:c0 + 128], in0=E[:, c0:c0 + 128], in1=tri[:, :], op=ALU.mult), reads=[En, 'tri'], writes=[En])
                    S.op('pe', I('matmul', pb[ob][:, c0:512], lhsT=Vaug[:, kt, hh, :], rhs=E[:, c0:512], start=(kt == 0), stop=(kt == nkt - 1),
                                 skip_group_check=True), reads=[En, 'Vaug'], writes=[pbn[ob]])
                    if kt == nkt - 1:
                        sl = slice(qb * 512, (qb + 1) * 512)
                        S.op('dve', I('reciprocal', out=rsb[64:128, :], in_=pb[ob][64:128, :]), reads=[pbn[ob]], writes=['rsb'])
                        S.op('dve', I('tensor_tensor', out=tb[r0:r0 + 64, :], in0=pb[ob][0:64, :], in1=rsb[64:128, :], op=ALU.mult), reads=[pbn[ob], 'rsb'], writes=['tb'])
                        S.op('dve', I('tensor_tensor', out=ybuf[r0:r0 + 64, sl], in0=tb[r0:r0 + 64, :], in1=Gb[r0:r0 + 64, sl], op=ALU.mult), reads=['tb', 'Gb'], writes=['ybuf'])
                for i in range(len(T) + LOOK):
                    if i < len(T): emit_S(i)
                    if i >= LOOK: emit_rest(i - LOOK)
            S.op('sp', I('dma_start', out=yT_d[j], in_=ybuf[:, :]), reads=['ybuf'], writes=['yT_d'], dma=True)
        S.barrier()
        if l == 0 and stop_here('B1'): return nc

        WORK.reset()
        wb = [WORK.alloc([128, 8, 512], BF16) for _ in range(2)]
        rot = [WORK.alloc([128, SEQ], BF16) for _ in range(2)]
        ctab = [WORK.alloc([128, 512], F32) for _ in range(2)]
        stab = [WORK.alloc([128, 512], F32) for _ in range(2)]
        t1 = WORK.alloc([128, 512], F32)
        t2 = WORK.alloc([128, 512], F32)
        load_wblock(wb[0], 'wb0', l, 4 * 512)
        load_wblock(wb[1], 'wb1', l, 5 * 512)
        for nb in range(NB):
            sl = slice(nb * 512, (nb + 1) * 512)
            b = nb % 2
            S.op('sp', I('dma_start', out=ctab[b][:, :], in_=rope_d[0][:, sl]), reads=['rope_d'], writes=['ctab%d' % b], dma=True)
            S.op('sp', I('dma_start', out=stab[b][:, :], in_=rope_d[1][:, sl]), reads=['rope_d'], writes=['stab%d' % b], dma=True)
            for qk in range(2):
                ba = (2 * qk) % 4; bb = (2 * qk + 1) % 4
                proj_fm(wb[0], 'wb0', qk * 256, 128, nb, ba)
                proj_fm(wb[0], 'wb0', qk * 256 + 128, 128, nb, bb)
                S.op('dve', I('tensor_tensor', out=t1[:, :], in0=pb[ba][:, :], in1=ctab[b][:, :], op=ALU.mult), reads=[pbn[ba], 'ctab%d' % b], writes=['t1'])
                S.op('dve', I('tensor_tensor', out=t2[:, :], in0=pb[bb][:, :], in1=stab[b][:, :], op=ALU.mult), reads=[pbn[bb], 'stab%d' % b], writes=['t2'])
                S.op('dve', I('tensor_tensor', out=rot[qk][:, sl], in0=t1[:, :], in1=t2[:, :], op=ALU.add), reads=['t1', 't2'], writes=['rot%d' % qk])
        for qk in range(2):
            S.op('sp', I('dma_start', out=rot_d[qk], in_=rot[qk][:, :]), reads=['rot%d' % qk], writes=['rot_d'], dma=True)
        S.barrier()
        if l == 0 and stop_here('B2'): return nc

        WORK.reset()
        wb = [WORK.alloc([128, 8, 512], BF16) for _ in range(2)]
        Qc = [WORK.alloc([128, SEQ], BF16) for _ in range(2)]
        Kc = [WORK.alloc([128, SEQ], BF16) for _ in range(2)]
        Vd = WORK.alloc([128, NT, 128], BF16)
        Gd = WORK.alloc([128, SEQ], BF16)
        ybuf = WORK.alloc([128, SEQ], BF16)
        Et = [WORK.alloc([128, 512], BF16) for _ in range(4)]
        rzb = WORK.alloc([128, 512], F32)
        tnb = [WORK.alloc([128, 512], F32) for _ in range(2)]
        obf = [WORK.alloc([128, 512], F32) for _ in range(2)]
        sqb = [WORK.alloc([128, 512], F32) for _ in range(2)]
        lnb = [WORK.alloc([128, 512], F32) for _ in range(2)]
        zacc = [[[WORK.alloc([128, 512], F32) for _ in range(2)] for _ in range(2)] for _ in range(2)]
        ecount = [0]
        for h in range(4):
            w = wb[(h + 1) % 2]; wn = 'wb%d' % ((h + 1) % 2)
            if h + 1 < 4:
                load_wblock(wb[h % 2], 'wb%d' % (h % 2), l, (6 + h) * 512)
            for nb in range(NB):
                sl = slice(nb * 512, (nb + 1) * 512)
                bq = (3 * nb) % 4; bk = (3 * nb + 1) % 4; bg = (3 * nb + 2) % 4
                proj_fm(w, wn, 0, 128, nb, bq)
                for c in range(2):
                    S.op('act', I('copy', out=Qc[c][0:64, sl], in_=pb[bq][64 * c:64 * c + 64, :]), reads=[pbn[bq]], writes=['Qc%d' % c])
                proj_fm(w, wn, 128, 128, nb, bk)
                for c in range(2):
                    S.op('dve', I('tensor_copy', out=Kc[c][0:64, sl], in_=pb[bk][64 * c:64 * c + 64, :]), reads=[pbn[bk]], writes=['Kc%d' % c])
                proj_fm(w, wn, 384, 128, nb, bg)
                S.op('act', I('activation', out=Gd[:, sl], in_=pb[bg][:, :], func=AF.Silu), reads=[pbn[bg]], writes=['Gd'])
            for c in range(2):
                g = 2 * h + c
                S.op('sp', I('dma_start', out=Qc[c][0:16, :], in_=rot_d[0][16 * g:16 * g + 16, :]), reads=['rot_d'], writes=['Qc%d' % c], dma=True)
                S.op('sp', I('dma_start', out=Kc[c][0:16, :], in_=rot_d[1][16 * g:16 * g + 16, :]), reads=['rot_d'], writes=['Kc%d' % c], dma=True)
            proj_tm(w, wn, 256,
                    lambda g, bank: I('tensor_copy', out=Vd[:, g * 4:(g + 1) * 4, :], in_=bank[:, :].rearrange("p (t d) -> p t d", t=4)), 'Vd')
            T = [(qb, c, kt) for qb in range(NB) for c in range(2) for kt in range(4 * (qb + 1))]
            info = {}
            SBK = [1, 4, 5]; OBK = [6, 7]; ZBK = [2, 3]; LOOK = 2; LAG = 3
            pending = []

            def emit_S(i):
                qb, c, kt = T[i]
                jd = kt - 4 * qb
                c0 = 128 * jd if jd > 0 else 0
                sb = SBK[ecount[0] % 3]; E = Et[ecount[0] % 4]; En = 'E%d' % (ecount[0] % 4); ecount[0] += 1
                info[i] = (sb, E, En, c0, jd)
                S.op('pe', I('matmul', pb[sb][:, c0:512], lhsT=Kc[c][0:64, kt * 128:(kt + 1) * 128],
                             rhs=Qc[c][0:64, qb * 512 + c0:(qb + 1) * 512], start=True, stop=True), reads=['Qc%d' % c, 'Kc%d' % c], writes=[pbn[sb]])

            def part2(qb):
                par2 = qb % 2; sl = slice(qb * 512, (qb + 1) * 512)
                ob_, sq_, ln_ = obf[par2], sqb[par2], lnb[par2]
                on_, sn_, lnn_ = 'obf%d' % par2, 'sqb%d' % par2, 'lnb%d' % par2
                S.op('pe', I('matmul', pb[0][:, :], lhsT=ones_f[:, :], rhs=sq_[:, :], start=True, stop=True), reads=[sn_, 'ones_f'], writes=[pbn[0]])
                S.op('act', I('activation', out=ln_[:, :], in_=pb[0][:, :], func=AF.Ln, bias=EPS, scale=1.0 / 128), reads=[pbn[0]], writes=[lnn_])
                S.op('act', I('activation', out=ln_[:, :], in_=ln_[:, :], func=AF.Exp, scale=-0.5), reads=[lnn_], writes=[lnn_])
                S.op('dve', I('tensor_tensor', out=ob_[:, :], in0=ob_[:, :], in1=ln_[:, :], op=ALU.mult), reads=[on_, lnn_], writes=[on_])
                S.op('dve', I('scalar_tensor_tensor', out=ybuf[:, sl], in0=ob_[:, :], scalar=small[:, 2:3], in1=Gd[:, sl], op0=ALU.mult, op1=ALU.mult),
                     reads=[on_, 'gprime', 'Gd'], writes=['ybuf'])

            def emit_rest(i):
                qb, c, kt = T[i]
                sb, E, En, c0, jd = info.pop(i)
                nkt = 4 * (qb + 1); ob = OBK[c]
                S.op('act', I('activation', out=E[:, c0:512], in_=pb[sb][:, c0:512], func=AF.Exp, scale=0.125), reads=[pbn[sb]], writes=[En])
                if jd >= 0:
                    S.op('dve', I('tensor_tensor', out=E[:, c0:c0 + 128], in0=E[:, c0:c0 + 128], in1=tri[:, :], op=ALU.mult), reads=[En, 'tri'], writes=[En])
                S.op('pe', I('matmul', pb[ob][:, c0:512], lhsT=Vd[:, kt, :], rhs=E[:, c0:512], start=(kt == 0), stop=(kt == nkt - 1),
                             skip_group_check=True), reads=[En, 'Vd'], writes=[pbn[ob]])
                zb = ZBK[c]
                S.op('pe', I('matmul', pb[zb][:, c0:512], lhsT=ones_bf[:, :], rhs=E[:, c0:512], start=(kt == 0), stop=(kt == nkt - 1),
                             skip_group_check=True), reads=[En, 'ones_bf'], writes=[pbn[zb]])
                if kt == nkt - 1:
                    par2 = qb % 2
                    S.op('dve', I('reciprocal', out=rzb[:, :], in_=pb[zb][:, :]), reads=[pbn[zb]], writes=['rzb'])
                    S.op('dve', I('tensor_tensor', out=tnb[c][:, :], in0=pb[ob][:, :], in1=rzb[:, :], op=ALU.mult), reads=[pbn[ob], 'rzb'], writes=['tnb%d' % c])
                    if c == 1:
                        ob_, sq_ = obf[par2], sqb[par2]
                        on_, sn_ = 'obf%d' % par2, 'sqb%d' % par2
                        S.op('dve', I('scalar_tensor_tensor', out=ob_[:, :], in0=tnb[1][:, :], scalar=small[:, 0:1], in1=tnb[0][:, :],
                                      op0=ALU.mult, op1=ALU.add), reads=['tnb0', 'tnb1', 'neglam'], writes=[on_])
                        S.op('dve', I('tensor_tensor', out=sq_[:, :], in0=ob_[:, :], in1=ob_[:, :], op=ALU.mult), reads=[on_], writes=[sn_])
                        pending.append((i + LAG, qb))
                while pending and pending[0][0] <= i:
                    part2(pending.pop(0)[1])
            for i in range(len(T) + LOOK):
                if i < len(T): emit_S(i)
                if i >= LOOK: emit_rest(i - LOOK)
            while pending:
                part2(pending.pop(0)[1])
            S.op('sp', I('dma_start', out=yT_d[4 + h], in_=ybuf[:, :]), reads=['ybuf'], writes=['yT_d'], dma=True)
        S.barrier()
        if l == 0 and stop_here('B3'): return nc

        WORK.reset()
        wb = [WORK.alloc([128, 8, 512], BF16) for _ in range(2)]
        ubuf = WORK.alloc([128, SEQ + 2], F32)
        ybuf = WORK.alloc([128, SEQ], BF16)
        cxs = WORK.alloc([128, 512], F32)
        acc = WORK.alloc([128, 512], F32)
        sg = WORK.alloc([128, 512], F32)
        S.op('dve', I('memset', ubuf[:, 0:2], 0.0), writes=['ubuf'])
        load_wblock(wb[0], 'wb0', l, 9 * 512)
        for j in range(4):
            w = wb[j % 2]; wn = 'wb%d' % (j % 2)
            if j + 1 < 4:
                load_wblock(wb[(j + 1) % 2], 'wb%d' % ((j + 1) % 2), l, (10 + j) * 512)
            cw = lambda k: P[:, P_CW + 3 * j + k:P_CW + 3 * j + k + 1]
            for nb in range(NB):
                sl = slice(nb * 512, (nb + 1) * 512)
                o4 = 4 * (nb % 2)
                bcb, bcc, bcx, bcg = o4, o4 + 1, o4 + 2, o4 + 3
                proj_fm(w, wn, 256, 128, nb, bcx)
                S.op('act', I('copy', out=cxs[:, :], in_=pb[bcx][:, :]), reads=[pbn[bcx]], writes=['cxs'])
                proj_fm(w, wn, 128, 128, nb, bcc)
                S.op('dve', I('tensor_tensor', out=ubuf[:, 2 + nb * 512:2 + (nb + 1) * 512], in0=pb[bcc][:, :], in1=cxs[:, :], op=ALU.mult),
                     reads=[pbn[bcc], 'cxs'], writes=['ubuf'])
                proj_fm(w, wn, 384, 128, nb, bcg)
                S.op('act', I('activation', out=sg[:, :], in_=pb[bcg][:, :], func=AF.Silu), reads=[pbn[bcg]], writes=['sg'])
                proj_fm(w, wn, 0, 128, nb, bcb)
                S.op('dve', I('tensor_scalar', out=acc[:, :], in0=ubuf[:, 2 + nb * 512:2 + (nb + 1) * 512], scalar1=cw(2), scalar2=None, op0=ALU.mult),
                     reads=['ubuf', pn], writes=['acc'])
                S.op('dve', I('scalar_tensor_tensor', out=acc[:, :], in0=ubuf[:, 1 + nb * 512:1 + (nb + 1) * 512], scalar=cw(1), in1=acc[:, :],
                              op0=ALU.mult, op1=ALU.add), reads=['ubuf', 'acc', pn], writes=['acc'])
                S.op('dve', I('scalar_tensor_tensor', out=acc[:, :], in0=ubuf[:, nb * 512:(nb + 1) * 512], scalar=cw(0), in1=acc[:, :],
                              op0=ALU.mult, op1=ALU.add), reads=['ubuf', 'acc', pn], writes=['acc'])
                S.op('dve', I('tensor_tensor', out=acc[:, :], in0=pb[bcb][:, :], in1=acc[:, :], op=ALU.mult), reads=[pbn[bcb], 'acc'], writes=['acc'])
                S.op('dve', I('tensor_tensor', out=ybuf[:, sl], in0=acc[:, :], in1=sg[:, :], op=ALU.mult), reads=['acc', 'sg'], writes=['ybuf'])
            S.op('sp', I('dma_start', out=yT_d[8 + j], in_=ybuf[:, :]), reads=['ybuf'], writes=['yT_d'], dma=True)
        S.barrier()
        if l == 0 and stop_here('B4'): return nc

        WORK.reset()
        wmg = WORK.alloc([128, 8, 3072], BF16)
        wbr = WORK.alloc([128, 12, D], BF16)
        wo = WORK.alloc([128, 8, D], BF16)
        ybk = WORK.alloc([128, 12, 512], BF16)
        mT = WORK.alloc([128, 8, 512], BF16)
        pgb = WORK.alloc([128, D], F32)
        xres = WORK.alloc([128, D], F32)
        xnew = WORK.alloc([128, D], F32)
        sgt = [WORK.alloc([128, 512], F32) for _ in range(3)]
        tt1 = WORK.alloc([128, 512], F32)
        tt2 = WORK.alloc([128, 512], F32)
        sso = WORK.alloc([128, 4], F32)
        for i in range(6):
            S.op('pool', I('dma_start', out=wmg[:, :, i * 512:(i + 1) * 512], in_=win[l][:, (13 + i) * 512:(14 + i) * 512].rearrange("(c p) n -> p c n", p=128)),
                 writes=['wmg'], dma=True)
        for i in range(3):
            S.op('pool', I('dma_start', out=wbr[:, 4 * i:4 * i + 4, :], in_=wbr_in[l][512 * i:512 * (i + 1), :].rearrange("(c p) n -> p c n", p=128)),
                 writes=['wbr'], dma=True)
        for i in range(2):
            S.op('pool', I('dma_start', out=wo[:, 4 * i:4 * i + 4, :], in_=wo_in[l][512 * i:512 * (i + 1), :].rearrange("(c p) n -> p c n", p=128)),
                 writes=['wo'], dma=True)
        S.op('sp', I('dma_start', out=pgb[:, :], in_=pg_in[l].partition_broadcast(128)), writes=['pgb'], dma=True)
        for nb in range(NB):
            sl = slice(nb * 512, (nb + 1) * 512)
            S.op('sp', I('dma_start', out=ybk[:, :, :], in_=yT_d[:, :, sl].rearrange("c p n -> p c n")), reads=['yT_d'], writes=['ybk'], dma=True)
            for oc in range(8):
                for br in range(3):
                    for kc in range(4):
                        S.op('pe', I('matmul', pb[br][:, :], lhsT=wbr[:, br * 4 + kc, oc * 128:(oc + 1) * 128], rhs=ybk[:, br * 4 + kc, :],
                                     start=(kc == 0), stop=(kc == 3)), reads=['wbr', 'ybk'], writes=[pbn[br]])
                for br in range(3):
                    c0 = br * 1024 + oc * 128
                    for kc in range(8):
                        S.op('pe', I('matmul', pb[3 + br][:, :], lhsT=wmg[:, kc, c0:c0 + 128], rhs=hT[:, kc, sl],
                                     start=(kc == 0), stop=(kc == 7)), reads=['wmg', 'hT'], writes=[pbn[3 + br]])
                    S.op('act', I('activation', out=sgt[br][:, :], in_=pb[3 + br][:, :], func=AF.Sigmoid, bias=P[:, P_BM + br * 8 + oc:P_BM + br * 8 + oc + 1]),
                         reads=[pbn[3 + br], pn], writes=['sgt%d' % br])
                S.op('dve', I('tensor_tensor', out=tt1[:, :], in0=pb[0][:, :], in1=sgt[0][:, :], op=ALU.mult), reads=[pbn[0], 'sgt0'], writes=['tt1'])
                S.op('dve', I('tensor_tensor', out=tt2[:, :], in0=pb[1][:, :], in1=sgt[1][:, :], op=ALU.mult), reads=[pbn[1], 'sgt1'], writes=['tt2'])
                S.op('dve', I('tensor_tensor', out=tt1[:, :], in0=tt1[:, :], in1=tt2[:, :], op=ALU.add), reads=['tt1', 'tt2'], writes=['tt1'])
                S.op('dve', I('tensor_tensor', out=tt2[:, :], in0=pb[2][:, :], in1=sgt[2][:, :], op=ALU.mult), reads=[pbn[2], 'sgt2'], writes=['tt2'])
                S.op('dve', I('tensor_tensor', out=mT[:, oc, :], in0=tt1[:, :], in1=tt2[:, :], op=ALU.add), reads=['tt1', 'tt2'], writes=['mT'])
            for ti in range(4):
                t = nb * 4 + ti
                S.op('sp', I('dma_start', out=xres[:, :], in_=xsrc[t * 128:(t + 1) * 128, :]), writes=['xres'], dma=True)
                S.op('dve', I('memset', sso[:, 0:2], 0.0), writes=['ssh0', 'ssh1'])
                for hf in range(2):
                    for kc in range(8):
                        S.op('pe', I('matmul', pb[6 + hf][:, :], lhsT=mT[:, kc, ti * 128:(ti + 1) * 128], rhs=wo[:, kc, hf * 512:(hf + 1) * 512],
                                     start=(kc == 0), stop=(kc == 7)), reads=['mT', 'wo'], writes=[pbn[6 + hf]])
                    S.op('act', I('activation', out=sgt[hf][:, :], in_=pb[6 + hf][:, :], func=AF.Square, accum_out=sso[:, hf:hf + 1]),
                         reads=[pbn[6 + hf], 'ssh%d' % hf], writes=['sgt%d' % hf, 'ssh%d' % hf])
                S.op('dve', I('tensor_tensor', out=sso[:, 2:3], in0=sso[:, 0:1], in1=sso[:, 1:2], op=ALU.add), reads=['ssh0', 'ssh1'], writes=['sso2'])
                S.op('act', I('activation', out=sso[:, 3:4], in_=sso[:, 2:3], func=AF.Sqrt, bias=EPS, scale=1.0 / D), reads=['sso2'], writes=['sso3'])
                S.op('dve', I('reciprocal', out=sso[:, 3:4], in_=sso[:, 3:4]), reads=['sso3'], writes=['sso3'])
                for hf in range(2):
                    hs = slice(hf * 512, (hf + 1) * 512)
                    S.op('dve', I('scalar_tensor_tensor', out=xnew[:, hs], in0=pb[6 + hf][:, :], scalar=sso[:, 3:4], in1=pgb[:, hs], op0=ALU.mult, op1=ALU.mult),
                         reads=[pbn[6 + hf], 'sso3', 'pgb'], writes=['xnew%d' % hf])
                    S.op('dve', I('tensor_tensor', out=xnew[:, hs], in0=xnew[:, hs], in1=xres[:, hs], op=ALU.add), reads=['xnew%d' % hf, 'xres'], writes=['xnew%d' % hf])
                S.op('sp', I('dma_start', out=xdst[t * 128:(t + 1) * 128, :], in_=xnew[:, :]), reads=['xnew0', 'xnew1'], writes=['xout%d_%d' % (l, t)], dma=True)
        S.barrier()
        if l == 0 and stop_here('C'): return nc
    S.barrier()
    S.emit(nc)
    return nc


O_FQ, O_FK, O_FV, O_FF, O_FG = 0, 512, 1024, 1536, 1544
O_DQ, O_DK, O_DV, O_DG = 2056, 2568, 3080, 3592
O_CB, O_CC, O_CX, O_CG, O_MG = 4104, 4616, 5128, 5640, 6152


def _col_index():
    idx = []
    r128 = np.arange(128)
    for j in range(4):
        for o in (O_FQ, O_FK, O_FV, O_FG):
            idx.append(o + j * 128 + r128)
    g = np.repeat(np.arange(8), 16); jj = np.tile(np.arange(16), 8)
    jsw = np.where(jj < 8, jj + 8, jj - 8)
    for o in (O_DQ, O_DK):
        idx.append(o + g * 64 + jj)
        idx.append(o + g * 64 + jsw)
    for h in range(4):
        for o in (O_DQ, O_DK, O_DV, O_DG):
            idx.append(o + h * 128 + r128)
    for j in range(4):
        for o in (O_CB, O_CC, O_CX, O_CG):
            idx.append(o + j * 128 + r128)
    idx.append(O_MG + np.arange(3072))
    idx.append(O_FF + np.arange(8))
    idx = np.concatenate(idx)
    assert idx.shape[0] == WCOLS
    return idx


_PROG = {}


def kernel(x, positions, pre_norm_g, w_in, b_forget, b_merge, conv_w, lam_q1, lam_k1, lam_q2, lam_k2,
           diff_norm_g, w_br_fox, w_br_diff, w_br_conv, w_out, post_norm_g):
    f32 = np.float32
    x = np.asarray(x, f32); positions = np.asarray(positions, np.int32)
    idx = _col_index()
    win = np.ascontiguousarray(np.asarray(w_in, f32)[:, :, idx])
    wbr = np.ascontiguousarray(np.concatenate([np.asarray(w_br_fox, f32), np.asarray(w_br_diff, f32), np.asarray(w_br_conv, f32)], axis=1))
    wo = np.ascontiguousarray(np.asarray(w_out, f32))
    par = np.zeros((DEPTH, 128, NPAR), f32)
    p = np.arange(128)
    inv_freq = (500000.0 ** (-np.arange(0, 16, 2, dtype=np.float32) / 16)).astype(f32)
    for l in range(DEPTH):
        par[l, :, P_G:P_G + 8] = np.asarray(pre_norm_g, f32)[l].reshape(8, 128).T
        par[l, :, P_BM:P_BM + 24] = np.asarray(b_merge, f32)[l].reshape(24, 128).T
        cw = np.asarray(conv_w, f32)[l]
        par[l, :, P_CW:P_CW + 12] = cw.reshape(3, 4, 128).transpose(2, 1, 0).reshape(128, 12)
        par[l, :, P_DG] = np.asarray(diff_norm_g, f32)[l]
        par[l, 0:8, P_BF] = np.asarray(b_forget, f32)[l]
        for i, v in enumerate((lam_q1, lam_k1, lam_q2, lam_k2)):
            par[l, :, P_LAM + 64 * i:P_LAM + 64 * (i + 1)] = np.asarray(v, f32)[l][None, :]
        par[l, :, P_ROPE] = inv_freq[(p % 16) % 8]
        par[l, :, P_ROPE + 1] = np.where((p % 16) < 8, -1.0, 1.0)
    postg = np.ascontiguousarray(np.asarray(post_norm_g, f32).reshape(DEPTH, 1, D))
    if 'nc' not in _PROG:
        _PROG['nc'] = build_program()
    nc = _PROG['nc']
    in_maps = []
    for c in range(NCORES):
        b = c % BATCH
        in_maps.append({"x": np.ascontiguousarray(x[b]), "pos": np.ascontiguousarray(positions[b:b + 1]), "win": win, "wbr": wbr,
                        "wo": wo, "par": par, "postg": postg})
    res = run_bass_kernel_spmd(nc, in_maps, core_ids=list(range(NCORES)))
    return np.stack([np.asarray(res.results[b]["out"], f32) for b in range(BATCH)], axis=0)
```

```python
import math
import numpy as np
from contextlib import ExitStack
import concourse.bass as bass
import concourse.mybir as mybir
from concourse.bass_utils import run_bass_kernel_spmd

F32 = mybir.dt.float32; BF16 = mybir.dt.bfloat16; I32 = mybir.dt.int32; U8 = mybir.dt.uint8
AF = mybir.ActivationFunctionType; ALU = mybir.AluOpType; AX = mybir.AxisListType

D = 1024; SEQ = 4096; BATCH = 4; DEPTH = 2; NCORES = 8
NT = SEQ // 128; NB = SEQ // 512
EPS = 1e-6
NBLK = 19
FFCOL = NBLK * 512
WCOLS = FFCOL + 8
NPAR = 8 + 24 + 12 + 1 + 1 + 256 + 2
P_G, P_BM, P_CW, P_DG, P_BF, P_LAM, P_ROPE = 0, 8, 32, 44, 45, 46, 302

ENGS = ('pe', 'dve', 'act', 'pool', 'sp')
CAP = 2000; NSLOT = 8; DCAP = 100


class Op:
    __slots__ = ('eng', 'fn', 'deps', 'sig', 'sem', 'val', 'dma', 'nobar')


class Sched:
    def __init__(self):
        self.ops = {e: [] for e in ENGS}
        self.lastw = {}
        self.readers = {}
        self.pending_dma = []

    def op(self, eng, fn, reads=(), writes=(), dma=False, nobar=False, extra=()):
        o = Op(); o.eng = eng; o.fn = fn; o.dma = dma; o.deps = []; o.sig = dma
        o.sem = None; o.val = None; o.nobar = nobar
        deps = set(extra)
        for r in reads:
            w = self.lastw.get(r)
            if w is not None: deps.add(w)
        for r in writes:
            w = self.lastw.get(r)
            if w is not None: deps.add(w)
            rd = self.readers.get(r)
            if rd:
                deps.update(rd[0].values()); deps.update(rd[1])
        for r in writes:
            self.lastw[r] = o; self.readers[r] = ({}, [])
        for r in reads:
            if r in writes: continue
            rd = self.readers.setdefault(r, ({}, []))
            if dma: rd[1].append(o)
            else: rd[0][eng] = o
        for d in deps:
            if d is o: continue
            if d.eng == 'pe' and eng == 'pe' and not d.dma and not dma: continue
            o.deps.append(d)
        self.ops[eng].append(o)
        if dma and not nobar: self.pending_dma.append(o)
        return o

    def barrier(self):
        last = [self.ops[e][-1] for e in ENGS if self.ops[e] and not self.ops[e][-1].dma]
        lastc = []
        for e in ENGS:
            for o in reversed(self.ops[e]):
                if not o.dma and o.fn is not None:
                    lastc.append(o); break
        ex = lastc + self.pending_dma
        self.pending_dma = []
        for e in ENGS:
            self.op(e, None, extra=ex)

    def emit(self, nc):
        for e in ENGS:
            for o in self.ops[e]:
                for d in o.deps: d.sig = True
        with ExitStack() as st:
            semcache = {}

            def getsem(key):
                if key not in semcache:
                    semcache[key] = st.enter_context(nc.semaphore("s_%s" % "_".join(map(str, key))))
                return semcache[key]
            for e in ENGS:
                k = 0; j = 0
                for o in self.ops[e]:
                    if o.fn is None: continue
                    if o.dma:
                        slot = j % NSLOT; n = j // NSLOT
                        o.sem = getsem(('d', e, slot, n // DCAP)); o.val = 16 * (n % DCAP + 1); j += 1
                    elif o.sig:
                        o.sem = getsem(('c', e, k // CAP)); o.val = k % CAP + 1; k += 1
            block = st.enter_context(nc.Block())

            def run(e):
                def body(eng):
                    seen = {}
                    for o in self.ops[e]:
                        need = {}
                        for d in o.deps:
                            if d.sem is None: continue
                            key = id(d.sem)
                            if need.get(key, (None, 0))[1] < d.val: need[key] = (d.sem, d.val)
                        for key, (sem, val) in need.items():
                            if seen.get(key, 0) < val:
                                eng.wait_ge(sem, val); seen[key] = val
                        if o.fn is None: continue
                        ins = o.fn(eng)
                        if o.dma: ins.then_inc(o.sem, 16)
                        elif o.sig: ins.then_inc(o.sem, 1)
                return body
            block.tensor(run('pe')); block.vector(run('dve')); block.scalar(run('act'))
            block.gpsimd(run('pool')); block.sync(run('sp'))


def I(name, *a, **kw):
    return lambda e: getattr(e, name)(*a, **kw)


_DTB = {F32: 4, BF16: 2, I32: 4}


def build_program(debug=False, stop=None):
    nc = bass.Bass("TRN2", target_bir_lowering=False)
    S = Sched()
    x_in = nc.dram_tensor("x", [SEQ, D], F32, kind="ExternalInput").ap()
    pos_in = nc.dram_tensor("pos", [1, SEQ], I32, kind="ExternalInput").ap()
    win = nc.dram_tensor("win", [DEPTH, D, WCOLS], F32, kind="ExternalInput").ap()
    wbr_in = nc.dram_tensor("wbr", [DEPTH, 1536, D], F32, kind="ExternalInput").ap()
    wo_in = nc.dram_tensor("wo", [DEPTH, D, D], F32, kind="ExternalInput").ap()
    par_in = nc.dram_tensor("par", [DEPTH, 128, NPAR], F32, kind="ExternalInput").ap()
    pg_in = nc.dram_tensor("postg", [DEPTH, 1, D], F32, kind="ExternalInput").ap()
    out = nc.dram_tensor("out", [SEQ, D], F32, kind="ExternalOutput").ap()
    sk = "ExternalOutput" if debug else "Internal"
    xs = nc.dram_tensor("xs", [SEQ, D], F32, kind=sk).ap()
    cs_d = nc.dram_tensor("cs_d", [2, 8, 3, SEQ], BF16, kind=sk).ap()
    rot_d = nc.dram_tensor("rot_d", [2, 128, SEQ], BF16, kind=sk).ap()
    rope_d = nc.dram_tensor("rope_d", [2, 128, SEQ], F32, kind=sk).ap()
    yT_d = nc.dram_tensor("yT_d", [12, 128, SEQ], BF16, kind=sk).ap()
    hT_d = nc.dram_tensor("hT_d", [128, 8, SEQ], BF16, kind=sk).ap() if debug else None

    def stop_here(name):
        if stop == name:
            if debug:
                S.op('sp', I('dma_start', out=hT_d, in_=hT[:, :, :]), reads=['hT'], writes=['hT_d'], dma=True)
            S.barrier(); S.emit(nc)
            return True
        return False

    TOTAL = 206 * 1024
    arena = nc.alloc_sbuf_tensor("arena", [128, TOTAL], U8)
    pb = [nc.alloc_psum_tensor("pb%d" % i, [128, 512], F32) for i in range(8)]
    pbn = ['pb%d' % i for i in range(8)]

    class Region:
        def __init__(self, base, size): self.base = base; self.size = size; self.off = 0
        def reset(self): self.off = 0
        def alloc(self, shape, dt):
            n = int(np.prod(shape[1:])) * _DTB[dt]
            n = (n + 31) // 32 * 32
            assert self.off + n <= self.size, ("region overflow", self.off, n, self.size)
            ap = arena[0:shape[0], self.base + self.off: self.base + self.off + int(np.prod(shape[1:])) * _DTB[dt]].bitcast(dt)
            if len(shape) == 3:
                ap = ap.rearrange("p (a b) -> p a b", a=shape[1])
            elif len(shape) == 4:
                ap = ap.rearrange("p (a b c) -> p a b c", a=shape[1], b=shape[2])
            self.off += n
            return ap

    CONST = Region(0, 6 * 1024)
    HT = Region(6 * 1024, 64 * 1024)
    WORK = Region(70 * 1024, TOTAL - 70 * 1024)

    ident = CONST.alloc([128, 128], BF16)
    ones_bf = CONST.alloc([128, 128], BF16)
    tri = CONST.alloc([128, 128], BF16)
    ones_f = CONST.alloc([128, 128], F32)
    par = [CONST.alloc([128, NPAR], F32) for _ in range(DEPTH)]
    small = CONST.alloc([128, 16], F32)
    hT = HT.alloc([128, 8, SEQ], BF16)

    S.op('pool', I('memset', ident[:, :], 1.0), writes=['ident'])
    S.op('pool', I('affine_select', out=ident[:, :], in_=ident[:, :], pattern=[[-1, 128]], compare_op=ALU.is_equal,
                   fill=0.0, base=0, channel_multiplier=1), reads=['ident'], writes=['ident'])
    S.op('pool', I('memset', ones_bf[:, :], 1.0), writes=['ones_bf'])
    S.op('pool', I('memset', ones_f[:, :], 1.0), writes=['ones_f'])
    S.op('pool', I('memset', tri[:, :], 1.0), writes=['tri'])
    S.op('pool', I('affine_select', out=tri[:, :], in_=tri[:, :], pattern=[[1, 128]], compare_op=ALU.is_ge,
                   fill=0.0, base=0, channel_multiplier=-1), reads=['tri'], writes=['tri'])
    for l in range(DEPTH):
        S.op('sp', I('dma_start', out=par[l][:, :], in_=par_in[l]), writes=['par%d' % l], dma=True)

    WORK.reset()
    posi = WORK.alloc([128, SEQ], I32)
    ang = WORK.alloc([128, SEQ], F32)
    tmp = WORK.alloc([128, SEQ], F32)
    r2 = WORK.alloc([128, SEQ], F32)
    TWO_PI = float(2 * np.pi); PI = float(np.pi)
    S.op('sp', I('dma_start', out=posi[:, :], in_=pos_in.partition_broadcast(128)), writes=['posi'], dma=True)
    S.op('dve', I('tensor_copy', out=ang[:, :], in_=posi[:, :]), reads=['posi'], writes=['ang'])
    S.op('dve', I('tensor_scalar', out=ang[:, :], in0=ang[:, :], scalar1=par[0][:, P_ROPE:P_ROPE + 1], scalar2=None, op0=ALU.mult),
         reads=['ang', 'par0'], writes=['ang'])
    S.op('dve', I('tensor_scalar', out=tmp[:, :], in0=ang[:, :], scalar1=float(1.0 / TWO_PI), scalar2=None, op0=ALU.mult),
         reads=['ang'], writes=['tmp'])
    S.op('dve', I('tensor_copy', out=posi[:, :], in_=tmp[:, :]), reads=['tmp'], writes=['posi'])
    S.op('dve', I('tensor_copy', out=tmp[:, :], in_=posi[:, :]), reads=['posi'], writes=['tmp'])
    S.op('dve', I('scalar_tensor_tensor', out=ang[:, :], in0=tmp[:, :], scalar=-TWO_PI, in1=ang[:, :], op0=ALU.mult, op1=ALU.add),
         reads=['tmp', 'ang'], writes=['ang'])
    S.op('dve', I('tensor_scalar', out=ang[:, :], in0=ang[:, :], scalar1=PI, scalar2=-PI, op0=ALU.min, op1=ALU.max),
         reads=['ang'], writes=['ang'])
    S.op('dve', I('tensor_scalar', out=r2[:, :], in0=ang[:, :], scalar1=float(PI / 2), scalar2=None, op0=ALU.add), reads=['ang'], writes=['r2'])
    S.op('dve', I('tensor_single_scalar', out=tmp[:, :], in_=r2[:, :], scalar=PI, op=ALU.is_gt), reads=['r2'], writes=['tmp'])
    S.op('dve', I('scalar_tensor_tensor', out=r2[:, :], in0=tmp[:, :], scalar=-TWO_PI, in1=r2[:, :], op0=ALU.mult, op1=ALU.add),
         reads=['tmp', 'r2'], writes=['r2'])
    S.op('dve', I('tensor_scalar', out=r2[:, :], in0=r2[:, :], scalar1=PI, scalar2=-PI, op0=ALU.min, op1=ALU.max), reads=['r2'], writes=['r2'])
    S.op('act', I('activation', out=tmp[:, :], in_=r2[:, :], func=AF.Sin), reads=['r2'], writes=['tmp'])
    S.op('sp', I('dma_start', out=rope_d[0], in_=tmp[:, :]), reads=['tmp'], writes=['rope_d'], dma=True)
    S.op('act', I('activation', out=r2[:, :], in_=ang[:, :], func=AF.Sin, scale=par[0][:, P_ROPE + 1:P_ROPE + 2]), reads=['ang', 'par0', 'r2'], writes=['r2'])
    S.op('sp', I('dma_start', out=rope_d[1], in_=r2[:, :]), reads=['r2'], writes=['rope_d'], dma=True)
    S.barrier()
    if stop_here('rope'): return nc

    wcount = [0]

    def load_wblock(wbuf, wname, l, col0, ncols=512):
        S.op('pool', I('dma_start', out=wbuf[:, :, 0:ncols], in_=win[l][:, col0:col0 + ncols].rearrange("(c p) n -> p c n", p=128)),
             writes=[wname], dma=True)

    def proj_fm(wbuf, wname, c0, M, nb, bank):
        for kc in range(8):
            S.op('pe', I('matmul', pb[bank][0:M, :], lhsT=wbuf[:, kc, c0:c0 + M], rhs=hT[:, kc, nb * 512:(nb + 1) * 512],
                         start=(kc == 0), stop=(kc == 7)), reads=[wname, 'hT'], writes=[pbn[bank]])

    def proj_tm(wbuf, wname, c0, dst_fn, dname):
        for g in range(NT // 4):
            bank = g % 4
            for i in range(4):
                tt = g * 4 + i
                for kc in range(8):
                    S.op('pe', I('matmul', pb[bank][:, i * 128:(i + 1) * 128], lhsT=hT[:, kc, tt * 128:(tt + 1) * 128],
                                 rhs=wbuf[:, kc, c0:c0 + 128], start=(kc == 0), stop=(kc == 7)), reads=[wname, 'hT'], writes=[pbn[bank]])
            S.op('dve', dst_fn(g, pb[bank]), reads=[pbn[bank]], writes=[dname])

    for l in range(DEPTH):
        lam_init = 0.8 - 0.6 * math.exp(-0.3 * l)
        P = par[l]; pn = 'par%d' % l
        xsrc = x_in if l == 0 else xs
        xdst = xs if l < DEPTH - 1 else out

        WORK.reset()
        xt = [WORK.alloc([128, D], F32) for _ in range(2)]
        xn = [WORK.alloc([128, D], BF16) for _ in range(2)]
        junk = WORK.alloc([128, D], BF16)
        ss = WORK.alloc([128, NT], F32)
        rs = WORK.alloc([128, NT], F32)
        S.op('dve', I('memset', ss[:, :], 0.0), writes=['ss'])
        for t in range(NT):
            b = t % 2
            S.op('sp', I('dma_start', out=xt[b][:, :], in_=xsrc[t * 128:(t + 1) * 128, :]), writes=['xt%d' % b], dma=True)
            S.op('act', I('activation', out=junk[:, :], in_=xt[b][:, :], func=AF.Square, accum_out=ss[:, t:t + 1]),
                 reads=['xt%d' % b, 'ss'], writes=['junk', 'ssc%d' % t])
            S.op('act', I('activation', out=rs[:, t:t + 1], in_=ss[:, t:t + 1], func=AF.Sqrt, bias=EPS, scale=1.0 / D),
                 reads=['ssc%d' % t], writes=['rs%d' % t])
            S.op('dve', I('reciprocal', out=rs[:, t:t + 1], in_=rs[:, t:t + 1]), reads=['rs%d' % t], writes=['rs%d' % t])
            S.op('act', I('mul', out=xn[b][:, :], in_=xt[b][:, :], mul=rs[:, t:t + 1]), reads=['xt%d' % b, 'rs%d' % t], writes=['xn%d' % b])
            pbt = pb[b][:, :].bitcast(BF16)
            for c in range(8):
                S.op('pe', I('transpose', pbt[:, c * 128:(c + 1) * 128], xn[b][:, c * 128:(c + 1) * 128], ident[:, :]),
                     reads=['xn%d' % b, 'ident'], writes=[pbn[b]])
            S.op('dve', I('tensor_tensor', out=hT[:, :, t * 128:(t + 1) * 128], in0=pbt.rearrange("p (c n) -> p c n", c=8),
                          in1=P[:, P_G:P_G + 8].unsqueeze(2).to_broadcast([128, 8, 128]), op=ALU.mult),
                 reads=[pbn[b], pn], writes=['hT'])
        S.barrier()
        if l == 0 and stop_here('A'): return nc

        WORK.reset()
        lt = WORK.alloc([128, 128], F32)
        S.op('dve', I('tensor_tensor', out=lt[:, 0:64], in0=P[:, P_LAM:P_LAM + 64], in1=P[:, P_LAM + 64:P_LAM + 128], op=ALU.mult), reads=[pn], writes=['lt'])
        S.op('dve', I('tensor_tensor', out=lt[:, 64:128], in0=P[:, P_LAM + 128:P_LAM + 192], in1=P[:, P_LAM + 192:P_LAM + 256], op=ALU.mult), reads=[pn, 'lt'], writes=['lt'])
        S.op('dve', I('tensor_reduce', out=small[:, 3:5], in_=lt[:, :].rearrange("p (a b) -> p a b", a=2), axis=AX.X, op=ALU.add), reads=['lt'], writes=['sm34'])
        S.op('act', I('activation', out=small[:, 5:7], in_=small[:, 3:5], func=AF.Exp), reads=['sm34'], writes=['sm56'])
        S.op('dve', I('tensor_tensor', out=small[:, 0:1], in0=small[:, 6:7], in1=small[:, 5:6], op=ALU.subtract), reads=['sm56'], writes=['neglam'])
        S.op('dve', I('tensor_scalar', out=small[:, 0:1], in0=small[:, 0:1], scalar1=float(-lam_init), scalar2=None, op0=ALU.add), reads=['neglam'], writes=['neglam'])
        S.op('dve', I('tensor_scalar', out=small[:, 1:2], in0=P[:, P_BF:P_BF + 1], scalar1=-1.0, scalar2=None, op0=ALU.mult), reads=[pn], writes=['negb'])
        S.op('dve', I('tensor_scalar', out=small[:, 2:3], in0=P[:, P_DG:P_DG + 1], scalar1=float(1.0 - lam_init), scalar2=None, op0=ALU.mult), reads=[pn], writes=['gprime'])

        wff = WORK.alloc([128, 8, 8], BF16)
        et = [WORK.alloc([8, 512], F32) for _ in range(2)]
        Lraw = WORK.alloc([8, SEQ], F32)
        Lc = WORK.alloc([8, SEQ], F32)
        csb = WORK.alloc([8, 3, SEQ], BF16)
        csn = WORK.alloc([8, 3, SEQ], BF16)
        S.op('pool', I('dma_start', out=wff[:, :, :], in_=win[l][:, FFCOL:FFCOL + 8].rearrange("(c p) n -> p c n", p=128)), writes=['wff'], dma=True)
        for nb in range(NB):
            b = nb % 2
            for kc in range(8):
                S.op('pe', I('matmul', pb[b][0:8, :], lhsT=wff[:, kc, 0:8], rhs=hT[:, kc, nb * 512:(nb + 1) * 512], start=(kc == 0), stop=(kc == 7)),
                     reads=['wff', 'hT'], writes=[pbn[b]])
            S.op('act', I('activation', out=et[b][:, :], in_=pb[b][0:8, :], func=AF.Exp, bias=small[0:8, 1:2], scale=-1.0),
                 reads=[pbn[b], 'negb'], writes=['et%d' % b])
            S.op('act', I('activation', out=Lraw[:, nb * 512:(nb + 1) * 512], in_=et[b][:, :], func=AF.Ln, bias=1.0, scale=1.0),
                 reads=['et%d' % b], writes=['Lraw'])
        S.op('dve', I('tensor_tensor_scan', out=Lc[:, :], data0=ones_f[0:8, 0:1].to_broadcast([8, SEQ]), data1=Lraw[:, :], initial=0.0,
                      op0=ALU.mult, op1=ALU.add), reads=['Lraw', 'ones_f'], writes=['Lc'])
        S.op('dve', I('tensor_copy', out=csb[:, 0, :], in_=Lc[:, :]), reads=['Lc'], writes=['csb'])
        S.op('dve', I('tensor_tensor', out=Lraw[:, :], in0=Lc[:, :], in1=csb[:, 0, :], op=ALU.subtract), reads=['Lc', 'csb'], writes=['Lraw'])
        S.op('dve', I('tensor_copy', out=csb[:, 1, :], in_=Lraw[:, :]), reads=['Lraw'], writes=['csb'])
        S.op('dve', I('tensor_tensor', out=Lc[:, :], in0=Lraw[:, :], in1=csb[:, 1, :], op=ALU.subtract), reads=['Lraw', 'csb'], writes=['Lc'])
        S.op('dve', I('tensor_copy', out=csb[:, 2, :], in_=Lc[:, :]), reads=['Lc'], writes=['csb'])
        S.op('dve', I('tensor_scalar', out=csn[:, :, :], in0=csb[:, :, :], scalar1=-1.0, scalar2=None, op0=ALU.mult), reads=['csb'], writes=['csn'])
        S.op('sp', I('dma_start', out=cs_d[0], in_=csb[:, :, :]), reads=['csb'], writes=['cs_d'], dma=True)
        S.op('sp', I('dma_start', out=cs_d[1], in_=csn[:, :, :]), reads=['csn'], writes=['cs_d'], dma=True)
        S.barrier()
        if l == 0 and stop_here('B0'): return nc

        WORK.reset()
        wb = [WORK.alloc([128, 8, 512], BF16) for _ in range(2)]
        QK = [WORK.alloc([128, SEQ], BF16) for _ in range(4)]
        Vaug = WORK.alloc([128, NT, 2, 128], BF16)
        Gb = WORK.alloc([128, SEQ], BF16)
        ybuf = WORK.alloc([128, SEQ], BF16)
        Et = [WORK.alloc([128, 512], BF16) for _ in range(4)]
        rsb = WORK.alloc([128, 512], F32)
        tb = WORK.alloc([128, 512], F32)
        for i in range(4):
            S.op('pool', I('memset', QK[i][64:70, :], 1.0), writes=['QK%d' % i])
        S.op('pool', I('memset', Vaug[:, :, :, 64:128], 1.0), writes=['Vaug'])
        load_wblock(wb[0], 'wb0', l, 0)
        ecount = [0]
        for j in range(4):
            w = wb[j % 2]; wn = 'wb%d' % (j % 2)
            if j + 1 < 4:
                load_wblock(wb[(j + 1) % 2], 'wb%d' % ((j + 1) % 2), l, (j + 1) * 512)
            for nb in range(NB):
                sl = slice(nb * 512, (nb + 1) * 512)
                bq = (2 * nb) % 4; bk = (2 * nb + 1) % 4
                proj_fm(w, wn, 0, 128, nb, bq)
                S.op('act', I('mul', out=QK[0][0:64, sl], in_=pb[bq][0:64, :], mul=0.125), reads=[pbn[bq]], writes=['QK0'])
                S.op('act', I('mul', out=QK[1][0:64, sl], in_=pb[bq][64:128, :], mul=0.125), reads=[pbn[bq]], writes=['QK1'])
                proj_fm(w, wn, 128, 128, nb, bk)
                S.op('dve', I('tensor_copy', out=QK[2][0:64, sl], in_=pb[bk][0:64, :]), reads=[pbn[bk]], writes=['QK2'])
                S.op('dve', I('tensor_copy', out=QK[3][0:64, sl], in_=pb[bk][64:128, :]), reads=[pbn[bk]], writes=['QK3'])
            for hh in range(2):
                head = 2 * j + hh
                S.op('sp', I('dma_start', out=QK[hh][64:67, :], in_=cs_d[1, head]), reads=['cs_d'], writes=['QK%d' % hh], dma=True)
                S.op('sp', I('dma_start', out=QK[2 + hh][67:70, :], in_=cs_d[0, head]), reads=['cs_d'], writes=['QK%d' % (2 + hh)], dma=True)
            for nb in range(NB):
                sl = slice(nb * 512, (nb + 1) * 512)
                bg = nb % 4
                proj_fm(w, wn, 384, 128, nb, bg)
                S.op('act', I('activation', out=Gb[:, sl], in_=pb[bg][:, :], func=AF.Silu), reads=[pbn[bg]], writes=['Gb'])
            proj_tm(w, wn, 256,
                    lambda g, bank: I('tensor_copy', out=Vaug[:, g * 4:(g + 1) * 4, :, 0:64],
                                      in_=bank[:, :].rearrange("p (t h d) -> p t h d", t=4, h=2)), 'Vaug')
            for hh in range(2):
                Qn, Kn = 'QK%d' % hh, 'QK%d' % (2 + hh)
                Qa, Ka = QK[hh], QK[2 + hh]
                r0 = 64 * hh
                T = [(qb, kt) for qb in range(NB) for kt in range(4 * (qb + 1))]
                info = {}
                SBK = [3, 4, 5]; LOOK = 2

                def emit_S(i):
                    qb, kt = T[i]
                    jd = kt - 4 * qb
                    c0 = 128 * jd if jd > 0 else 0
                    sb = SBK[ecount[0] % 3]; E = Et[ecount[0] % 4]; En = 'E%d' % (ecount[0] % 4); ecount[0] += 1
                    info[i] = (sb, E, En, c0, jd)
                    S.op('pe', I('matmul', pb[sb][:, c0:512], lhsT=Ka[0:70, kt * 128:(kt + 1) * 128], rhs=Qa[0:70, qb * 512 + c0:(qb + 1) * 512],
                                 start=True, stop=True), reads=[Qn, Kn], writes=[pbn[sb]])

                def emit_rest(i):
                    qb, kt = T[i]
                    sb, E, En, c0, jd = info.pop(i)
                    nkt = 4 * (qb + 1); ob = 6 + (qb % 2)
                    S.op('act', I('activation', out=E[:, c0:512], in_=pb[sb][:, c0:512], func=AF.Exp), reads=[pbn[sb]], writes=[En])
                    if jd >= 0:
                        S.op('dve', I('tensor_tensor', out=E[:, c0:c0 + 128], in0=E[:, c0:c0 + 128], in1=tri[:, :], op=ALU.mult), reads=[En, 'tri'], writes=[En])
                    S.op('pe', I('matmul', pb[ob][:, c0:512], lhsT=Vaug[:, kt, hh, :], rhs=E[:, c0:512], start=(kt == 0), stop=(kt == nkt - 1),
                                 skip_group_check=True), reads=[En, 'Vaug'], writes=[pbn[ob]])
                    if kt == nkt - 1:
                        sl = slice(qb * 512, (qb + 1) * 512)
                        S.op('dve', I('reciprocal', out=rsb[64:128, :], in_=pb[ob][64:128, :]), reads=[pbn[ob]], writes=['rsb'])
                        S.op('dve', I('tensor_tensor', out=tb[r0:r0 + 64, :], in0=pb[ob][0:64, :], in1=rsb[64:128, :], op=ALU.mult), reads=[pbn[ob], 'rsb'], writes=['tb'])
                        S.op('dve', I('tensor_tensor', out=ybuf[r0:r0 + 64, sl], in0=tb[r0:r0 + 64, :], in1=Gb[r0:r0 + 64, sl], op=ALU.mult), reads=['tb', 'Gb'], writes=['ybuf'])
                for i in range(len(T) + LOOK):
                    if i < len(T): emit_S(i)
                    if i >= LOOK: emit_rest(i - LOOK)
            S.op('sp', I('dma_start', out=yT_d[j], in_=ybuf[:, :]), reads=['ybuf'], writes=['yT_d'], dma=True)
        S.barrier()
        if l == 0 and stop_here('B1'): return nc

        WORK.reset()
        wb = [WORK.alloc([128, 8, 512], BF16) for _ in range(2)]
        rot = [WORK.alloc([128, SEQ], BF16) for _ in range(2)]
        ctab = [WORK.alloc([128, 512], F32) for _ in range(2)]
        stab = [WORK.alloc([128, 512], F32) for _ in range(2)]
        t1 = WORK.alloc([128, 512], F32)
        t2 = WORK.alloc([128, 512], F32)
        load_wblock(wb[0], 'wb0', l, 4 * 512)
        load_wblock(wb[1], 'wb1', l, 5 * 512)
        for nb in range(NB):
            sl = slice(nb * 512, (nb + 1) * 512)
            b = nb % 2
            S.op('sp', I('dma_start', out=ctab[b][:, :], in_=rope_d[0][:, sl]), reads=['rope_d'], writes=['ctab%d' % b], dma=True)
            S.op('sp', I('dma_start', out=stab[b][:, :], in_=rope_d[1][:, sl]), reads=['rope_d'], writes=['stab%d' % b], dma=True)
            for qk in range(2):
                ba = (2 * qk) % 4; bb = (2 * qk + 1) % 4
                proj_fm(wb[0], 'wb0', qk * 256, 128, nb, ba)
                proj_fm(wb[0], 'wb0', qk * 256 + 128, 128, nb, bb)
                S.op('dve', I('tensor_tensor', out=t1[:, :], in0=pb[ba][:, :], in1=ctab[b][:, :], op=ALU.mult), reads=[pbn[ba], 'ctab%d' % b], writes=['t1'])
                S.op('dve', I('tensor_tensor', out=t2[:, :], in0=pb[bb][:, :], in1=stab[b][:, :], op=ALU.mult), reads=[pbn[bb], 'stab%d' % b], writes=['t2'])
                S.op('dve', I('tensor_tensor', out=rot[qk][:, sl], in0=t1[:, :], in1=t2[:, :], op=ALU.add), reads=['t1', 't2'], writes=['rot%d' % qk])
        for qk in range(2):
            S.op('sp', I('dma_start', out=rot_d[qk], in_=rot[qk][:, :]), reads=['rot%d' % qk], writes=['rot_d'], dma=True)
        S.barrier()
        if l == 0 and stop_here('B2'): return nc

        WORK.reset()
        wb = [WORK.alloc([128, 8, 512], BF16) for _ in range(2)]
        Qc = [WORK.alloc([128, SEQ], BF16) for _ in range(2)]
        Kc = [WORK.alloc([128, SEQ], BF16) for _ in range(2)]
        Vd = WORK.alloc([128, NT, 128], BF16)
        Gd = WORK.alloc([128, SEQ], BF16)
        ybuf = WORK.alloc([128, SEQ], BF16)
        Et = [WORK.alloc([128, 512], BF16) for _ in range(4)]
        rzb = WORK.alloc([128, 512], F32)
        tnb = [WORK.alloc([128, 512], F32) for _ in range(2)]
        obf = [WORK.alloc([128, 512], F32) for _ in range(2)]
        sqb = [WORK.alloc([128, 512], F32) for _ in range(2)]
        lnb = [WORK.alloc([128, 512], F32) for _ in range(2)]
        sqh = [[WORK.alloc([128, 512], BF16) for _ in range(2)] for _ in range(2)]
        zacc = [[[WORK.alloc([128, 512], F32) for _ in range(2)] for _ in range(2)] for _ in range(2)]
        ecount = [0]
        for h in range(4):
            w = wb[(h + 1) % 2]; wn = 'wb%d' % ((h + 1) % 2)
            if h + 1 < 4:
                load_wblock(wb[h % 2], 'wb%d' % (h % 2), l, (6 + h) * 512)
            for nb in range(NB):
                sl = slice(nb * 512, (nb + 1) * 512)
                bq = (3 * nb) % 4; bk = (3 * nb + 1) % 4; bg = (3 * nb + 2) % 4
                proj_fm(w, wn, 0, 128, nb, bq)
                for c in range(2):
                    S.op('act', I('copy', out=Qc[c][0:64, sl], in_=pb[bq][64 * c:64 * c + 64, :]), reads=[pbn[bq]], writes=['Qc%d' % c])
                proj_fm(w, wn, 128, 128, nb, bk)
                for c in range(2):
                    S.op('dve', I('tensor_copy', out=Kc[c][0:64, sl], in_=pb[bk][64 * c:64 * c + 64, :]), reads=[pbn[bk]], writes=['Kc%d' % c])
                proj_fm(w, wn, 384, 128, nb, bg)
                S.op('act', I('activation', out=Gd[:, sl], in_=pb[bg][:, :], func=AF.Silu), reads=[pbn[bg]], writes=['Gd'])
            for c in range(2):
                g = 2 * h + c
                S.op('sp', I('dma_start', out=Qc[c][0:16, :], in_=rot_d[0][16 * g:16 * g + 16, :]), reads=['rot_d'], writes=['Qc%d' % c], dma=True)
                S.op('sp', I('dma_start', out=Kc[c][0:16, :], in_=rot_d[1][16 * g:16 * g + 16, :]), reads=['rot_d'], writes=['Kc%d' % c], dma=True)
            proj_tm(w, wn, 256,
                    lambda g, bank: I('tensor_copy', out=Vd[:, g * 4:(g + 1) * 4, :], in_=bank[:, :].rearrange("p (t d) -> p t d", t=4)), 'Vd')
            T = [(qb, c, kt) for qb in range(NB) for c in range(2) for kt in range(4 * (qb + 1))]
            info = {}
            SBK = [1, 4, 5]; OBK = [6, 7]; ZBK = [2, 3]; LOOK = 2; LAG = 3
            pending = []

            def emit_S(i):
                qb, c, kt = T[i]
                jd = kt - 4 * qb
                c0 = 128 * jd if jd > 0 else 0
                sb = SBK[ecount[0] % 3]; E = Et[ecount[0] % 4]; En = 'E%d' % (ecount[0] % 4); ecount[0] += 1
                info[i] = (sb, E, En, c0, jd)
                S.op('pe', I('matmul', pb[sb][:, c0:512], lhsT=Kc[c][0:64, kt * 128:(kt + 1) * 128],
                             rhs=Qc[c][0:64, qb * 512 + c0:(qb + 1) * 512], start=True, stop=True), reads=['Qc%d' % c, 'Kc%d' % c], writes=[pbn[sb]])

            def part2(qb):
                par2 = qb % 2; sl = slice(qb * 512, (qb + 1) * 512)
                ob_, sq_, ln_ = obf[par2], sqb[par2], lnb[par2]
                on_, sn_, lnn_ = 'obf%d' % par2, 'sqb%d' % par2, 'lnb%d' % par2
                hi_, lo_ = sqh[par2][0], sqh[par2][1]
                S.op('dve', I('tensor_copy', out=hi_[:, :], in_=sq_[:, :]), reads=[sn_], writes=['sqh%d' % par2])
                S.op('dve', I('tensor_tensor', out=lo_[:, :], in0=sq_[:, :], in1=hi_[:, :], op=ALU.subtract), reads=[sn_, 'sqh%d' % par2], writes=['sql%d' % par2])
                S.op('pe', I('matmul', pb[0][:, :], lhsT=ones_bf[:, :], rhs=hi_[:, :], start=True, stop=False), reads=['sqh%d' % par2, 'ones_bf'], writes=[pbn[0]])
                S.op('pe', I('matmul', pb[0][:, :], lhsT=ones_bf[:, :], rhs=lo_[:, :], start=False, stop=True), reads=['sql%d' % par2, 'ones_bf'], writes=[pbn[0]])
                S.op('act', I('activation', out=ln_[:, :], in_=pb[0][:, :], func=AF.Ln, bias=EPS, scale=1.0 / 128), reads=[pbn[0]], writes=[lnn_])
                S.op('act', I('activation', out=ln_[:, :], in_=ln_[:, :], func=AF.Exp, scale=-0.5), reads=[lnn_], writes=[lnn_])
                S.op('dve', I('tensor_tensor', out=ob_[:, :], in0=ob_[:, :], in1=ln_[:, :], op=ALU.mult), reads=[on_, lnn_], writes=[on_])
                S.op('dve', I('scalar_tensor_tensor', out=ybuf[:, sl], in0=ob_[:, :], scalar=small[:, 2:3], in1=Gd[:, sl], op0=ALU.mult, op1=ALU.mult),
                     reads=[on_, 'gprime', 'Gd'], writes=['ybuf'])

            def emit_rest(i):
                qb, c, kt = T[i]
                sb, E, En, c0, jd = info.pop(i)
                nkt = 4 * (qb + 1); ob = OBK[c]
                S.op('act', I('activation', out=E[:, c0:512], in_=pb[sb][:, c0:512], func=AF.Exp, scale=0.125), reads=[pbn[sb]], writes=[En])
                if jd >= 0:
                    S.op('dve', I('tensor_tensor', out=E[:, c0:c0 + 128], in0=E[:, c0:c0 + 128], in1=tri[:, :], op=ALU.mult), reads=[En, 'tri'], writes=[En])
                S.op('pe', I('matmul', pb[ob][:, c0:512], lhsT=Vd[:, kt, :], rhs=E[:, c0:512], start=(kt == 0), stop=(kt == nkt - 1),
                             skip_group_check=True), reads=[En, 'Vd'], writes=[pbn[ob]])
                zb = ZBK[c]
                S.op('pe', I('matmul', pb[zb][:, c0:512], lhsT=ones_bf[:, :], rhs=E[:, c0:512], start=(kt == 0), stop=(kt == nkt - 1),
                             skip_group_check=True), reads=[En, 'ones_bf'], writes=[pbn[zb]])
                if kt == nkt - 1:
                    par2 = qb % 2
                    S.op('dve', I('reciprocal', out=rzb[:, :], in_=pb[zb][:, :]), reads=[pbn[zb]], writes=['rzb'])
                    S.op('dve', I('tensor_tensor', out=tnb[c][:, :], in0=pb[ob][:, :], in1=rzb[:, :], op=ALU.mult), reads=[pbn[ob], 'rzb'], writes=['tnb%d' % c])
                    if c == 1:
                        ob_, sq_ = obf[par2], sqb[par2]
                        on_, sn_ = 'obf%d' % par2, 'sqb%d' % par2
                        S.op('dve', I('scalar_tensor_tensor', out=ob_[:, :], in0=tnb[1][:, :], scalar=small[:, 0:1], in1=tnb[0][:, :],
                                      op0=ALU.mult, op1=ALU.add), reads=['tnb0', 'tnb1', 'neglam'], writes=[on_])
                        S.op('dve', I('tensor_tensor', out=sq_[:, :], in0=ob_[:, :], in1=ob_[:, :], op=ALU.mult), reads=[on_], writes=[sn_])
                        pending.append((i + LAG, qb))
                while pending and pending[0][0] <= i:
                    part2(pending.pop(0)[1])
            for i in range(len(T) + LOOK):
                if i < len(T): emit_S(i)
                if i >= LOOK: emit_rest(i - LOOK)
            while pending:
                part2(pending.pop(0)[1])
            S.op('sp', I('dma_start', out=yT_d[4 + h], in_=ybuf[:, :]), reads=['ybuf'], writes=['yT_d'], dma=True)
        S.barrier()
        if l == 0 and stop_here('B3'): return nc

        WORK.reset()
        wb = [WORK.alloc([128, 8, 512], BF16) for _ in range(2)]
        ubuf = WORK.alloc([128, SEQ + 2], F32)
        ybuf = WORK.alloc([128, SEQ], BF16)
        cxs = WORK.alloc([128, 512], F32)
        acc = WORK.alloc([128, 512], F32)
        sg = WORK.alloc([128, 512], F32)
        S.op('dve', I('memset', ubuf[:, 0:2], 0.0), writes=['ubuf'])
        load_wblock(wb[0], 'wb0', l, 9 * 512)
        for j in range(4):
            w = wb[j % 2]; wn = 'wb%d' % (j % 2)
            if j + 1 < 4:
                load_wblock(wb[(j + 1) % 2], 'wb%d' % ((j + 1) % 2), l, (10 + j) * 512)
            cw = lambda k: P[:, P_CW + 3 * j + k:P_CW + 3 * j + k + 1]
            for nb in range(NB):
                sl = slice(nb * 512, (nb + 1) * 512)
                o4 = 4 * (nb % 2)
                bcb, bcc, bcx, bcg = o4, o4 + 1, o4 + 2, o4 + 3
                proj_fm(w, wn, 256, 128, nb, bcx)
                S.op('act', I('copy', out=cxs[:, :], in_=pb[bcx][:, :]), reads=[pbn[bcx]], writes=['cxs'])
                proj_fm(w, wn, 128, 128, nb, bcc)
                S.op('dve', I('tensor_tensor', out=ubuf[:, 2 + nb * 512:2 + (nb + 1) * 512], in0=pb[bcc][:, :], in1=cxs[:, :], op=ALU.mult),
                     reads=[pbn[bcc], 'cxs'], writes=['ubuf'])
                proj_fm(w, wn, 384, 128, nb, bcg)
                S.op('act', I('activation', out=sg[:, :], in_=pb[bcg][:, :], func=AF.Silu), reads=[pbn[bcg]], writes=['sg'])
                proj_fm(w, wn, 0, 128, nb, bcb)
                S.op('dve', I('tensor_scalar', out=acc[:, :], in0=ubuf[:, 2 + nb * 512:2 + (nb + 1) * 512], scalar1=cw(2), scalar2=None, op0=ALU.mult),
                     reads=['ubuf', pn], writes=['acc'])
                S.op('dve', I('scalar_tensor_tensor', out=acc[:, :], in0=ubuf[:, 1 + nb * 512:1 + (nb + 1) * 512], scalar=cw(1), in1=acc[:, :],
                              op0=ALU.mult, op1=ALU.add), reads=['ubuf', 'acc', pn], writes=['acc'])
                S.op('dve', I('scalar_tensor_tensor', out=acc[:, :], in0=ubuf[:, nb * 512:(nb + 1) * 512], scalar=cw(0), in1=acc[:, :],
                              op0=ALU.mult, op1=ALU.add), reads=['ubuf', 'acc', pn], writes=['acc'])
                S.op('dve', I('tensor_tensor', out=acc[:, :], in0=pb[bcb][:, :], in1=acc[:, :], op=ALU.mult), reads=[pbn[bcb], 'acc'], writes=['acc'])
                S.op('dve', I('tensor_tensor', out=ybuf[:, sl], in0=acc[:, :], in1=sg[:, :], op=ALU.mult), reads=['acc', 'sg'], writes=['ybuf'])
            S.op('sp', I('dma_start', out=yT_d[8 + j], in_=ybuf[:, :]), reads=['ybuf'], writes=['yT_d'], dma=True)
        S.barrier()
        if l == 0 and stop_here('B4'): return nc

        WORK.reset()
        wmg = WORK.alloc([128, 8, 3072], BF16)
        wbr = WORK.alloc([128, 12, D], BF16)
        wo = WORK.alloc([128, 8, D], BF16)
        ybk = WORK.alloc([128, 12, 512], BF16)
        mT = WORK.alloc([128, 8, 512], BF16)
        pgb = WORK.alloc([128, D], F32)
        xres = WORK.alloc([128, D], F32)
        xnew = WORK.alloc([128, D], F32)
        sgt = [WORK.alloc([128, 512], F32) for _ in range(3)]
        tt1 = WORK.alloc([128, 512], F32)
        tt2 = WORK.alloc([128, 512], F32)
        sso = WORK.alloc([128, 4], F32)
        for i in range(6):
            S.op('pool', I('dma_start', out=wmg[:, :, i * 512:(i + 1) * 512], in_=win[l][:, (13 + i) * 512:(14 + i) * 512].rearrange("(c p) n -> p c n", p=128)),
                 writes=['wmg'], dma=True)
        for i in range(3):
            S.op('pool', I('dma_start', out=wbr[:, 4 * i:4 * i + 4, :], in_=wbr_in[l][512 * i:512 * (i + 1), :].rearrange("(c p) n -> p c n", p=128)),
                 writes=['wbr'], dma=True)
        for i in range(2):
            S.op('pool', I('dma_start', out=wo[:, 4 * i:4 * i + 4, :], in_=wo_in[l][512 * i:512 * (i + 1), :].rearrange("(c p) n -> p c n", p=128)),
                 writes=['wo'], dma=True)
        S.op('sp', I('dma_start', out=pgb[:, :], in_=pg_in[l].partition_broadcast(128)), writes=['pgb'], dma=True)
        for nb in range(NB):
            sl = slice(nb * 512, (nb + 1) * 512)
            S.op('sp', I('dma_start', out=ybk[:, :, :], in_=yT_d[:, :, sl].rearrange("c p n -> p c n")), reads=['yT_d'], writes=['ybk'], dma=True)
            for oc in range(8):
                for br in range(3):
                    for kc in range(4):
                        S.op('pe', I('matmul', pb[br][:, :], lhsT=wbr[:, br * 4 + kc, oc * 128:(oc + 1) * 128], rhs=ybk[:, br * 4 + kc, :],
                                     start=(kc == 0), stop=(kc == 3)), reads=['wbr', 'ybk'], writes=[pbn[br]])
                for br in range(3):
                    c0 = br * 1024 + oc * 128
                    for kc in range(8):
                        S.op('pe', I('matmul', pb[3 + br][:, :], lhsT=wmg[:, kc, c0:c0 + 128], rhs=hT[:, kc, sl],
                                     start=(kc == 0), stop=(kc == 7)), reads=['wmg', 'hT'], writes=[pbn[3 + br]])
                    S.op('act', I('activation', out=sgt[br][:, :], in_=pb[3 + br][:, :], func=AF.Sigmoid, bias=P[:, P_BM + br * 8 + oc:P_BM + br * 8 + oc + 1]),
                         reads=[pbn[3 + br], pn], writes=['sgt%d' % br])
                S.op('dve', I('tensor_tensor', out=tt1[:, :], in0=pb[0][:, :], in1=sgt[0][:, :], op=ALU.mult), reads=[pbn[0], 'sgt0'], writes=['tt1'])
                S.op('dve', I('tensor_tensor', out=tt2[:, :], in0=pb[1][:, :], in1=sgt[1][:, :], op=ALU.mult), reads=[pbn[1], 'sgt1'], writes=['tt2'])
                S.op('dve', I('tensor_tensor', out=tt1[:, :], in0=tt1[:, :], in1=tt2[:, :], op=ALU.add), reads=['tt1', 'tt2'], writes=['tt1'])
                S.op('dve', I('tensor_tensor', out=tt2[:, :], in0=pb[2][:, :], in1=sgt[2][:, :], op=ALU.mult), reads=[pbn[2], 'sgt2'], writes=['tt2'])
                S.op('dve', I('tensor_tensor', out=mT[:, oc, :], in0=tt1[:, :], in1=tt2[:, :], op=ALU.add), reads=['tt1', 'tt2'], writes=['mT'])
            for ti in range(4):
                t = nb * 4 + ti
                S.op('sp', I('dma_start', out=xres[:, :], in_=xsrc[t * 128:(t + 1) * 128, :]), writes=['xres'], dma=True)
                S.op('dve', I('memset', sso[:, 0:2], 0.0), writes=['ssh0', 'ssh1'])
                for hf in range(2):
                    for kc in range(8):
                        S.op('pe', I('matmul', pb[6 + hf][:, :], lhsT=mT[:, kc, ti * 128:(ti + 1) * 128], rhs=wo[:, kc, hf * 512:(hf + 1) * 512],
                                     start=(kc == 0), stop=(kc == 7)), reads=['mT', 'wo'], writes=[pbn[6 + hf]])
                    S.op('act', I('activation', out=sgt[hf][:, :], in_=pb[6 + hf][:, :], func=AF.Square, accum_out=sso[:, hf:hf + 1]),
                         reads=[pbn[6 + hf], 'ssh%d' % hf], writes=['sgt%d' % hf, 'ssh%d' % hf])
                S.op('dve', I('tensor_tensor', out=sso[:, 2:3], in0=sso[:, 0:1], in1=sso[:, 1:2], op=ALU.add), reads=['ssh0', 'ssh1'], writes=['sso2'])
                S.op('act', I('activation', out=sso[:, 3:4], in_=sso[:, 2:3], func=AF.Sqrt, bias=EPS, scale=1.0 / D), reads=['sso2'], writes=['sso3'])
                S.op('dve', I('reciprocal', out=sso[:, 3:4], in_=sso[:, 3:4]), reads=['sso3'], writes=['sso3'])
                for hf in range(2):
                    hs = slice(hf * 512, (hf + 1) * 512)
                    S.op('dve', I('scalar_tensor_tensor', out=xnew[:, hs], in0=pb[6 + hf][:, :], scalar=sso[:, 3:4], in1=pgb[:, hs], op0=ALU.mult, op1=ALU.mult),
                         reads=[pbn[6 + hf], 'sso3', 'pgb'], writes=['xnew%d' % hf])
                    S.op('dve', I('tensor_tensor', out=xnew[:, hs], in0=xnew[:, hs], in1=xres[:, hs], op=ALU.add), reads=['xnew%d' % hf, 'xres'], writes=['xnew%d' % hf])
                S.op('sp', I('dma_start', out=xdst[t * 128:(t + 1) * 128, :], in_=xnew[:, :]), reads=['xnew0', 'xnew1'], writes=['xout%d_%d' % (l, t)], dma=True)
        S.barrier()
        if l == 0 and stop_here('C'): return nc
    S.barrier()
    S.emit(nc)
    return nc


O_FQ, O_FK, O_FV, O_FF, O_FG = 0, 512, 1024, 1536, 1544
O_DQ, O_DK, O_DV, O_DG = 2056, 2568, 3080, 3592
O_CB, O_CC, O_CX, O_CG, O_MG = 4104, 4616, 5128, 5640, 6152


def _col_index():
    idx = []
    r128 = np.arange(128)
    for j in range(4):
        for o in (O_FQ, O_FK, O_FV, O_FG):
            idx.append(o + j * 128 + r128)
    g = np.repeat(np.arange(8), 16); jj = np.tile(np.arange(16), 8)
    jsw = np.where(jj < 8, jj + 8, jj - 8)
    for o in (O_DQ, O_DK):
        idx.append(o + g * 64 + jj)
        idx.append(o + g * 64 + jsw)
    for h in range(4):
        for o in (O_DQ, O_DK, O_DV, O_DG):
            idx.append(o + h * 128 + r128)
    for j in range(4):
        for o in (O_CB, O_CC, O_CX, O_CG):
            idx.append(o + j * 128 + r128)
    idx.append(O_MG + np.arange(3072))
    idx.append(O_FF + np.arange(8))
    idx = np.concatenate(idx)
    assert idx.shape[0] == WCOLS
    return idx


_PROG = {}


def kernel(x, positions, pre_norm_g, w_in, b_forget, b_merge, conv_w, lam_q1, lam_k1, lam_q2, lam_k2,
           diff_norm_g, w_br_fox, w_br_diff, w_br_conv, w_out, post_norm_g):
    f32 = np.float32
    x = np.asarray(x, f32); positions = np.asarray(positions, np.int32)
    idx = _col_index()
    win = np.ascontiguousarray(np.asarray(w_in, f32)[:, :, idx])
    wbr = np.ascontiguousarray(np.concatenate([np.asarray(w_br_fox, f32), np.asarray(w_br_diff, f32), np.asarray(w_br_conv, f32)], axis=1))
    wo = np.ascontiguousarray(np.asarray(w_out, f32))
    par = np.zeros((DEPTH, 128, NPAR), f32)
    p = np.arange(128)
    inv_freq = (500000.0 ** (-np.arange(0, 16, 2, dtype=np.float32) / 16)).astype(f32)
    for l in range(DEPTH):
        par[l, :, P_G:P_G + 8] = np.asarray(pre_norm_g, f32)[l].reshape(8, 128).T
        par[l, :, P_BM:P_BM + 24] = np.asarray(b_merge, f32)[l].reshape(24, 128).T
        cw = np.asarray(conv_w, f32)[l]
        par[l, :, P_CW:P_CW + 12] = cw.reshape(3, 4, 128).transpose(2, 1, 0).reshape(128, 12)
        par[l, :, P_DG] = np.asarray(diff_norm_g, f32)[l]
        par[l, 0:8, P_BF] = np.asarray(b_forget, f32)[l]
        for i, v in enumerate((lam_q1, lam_k1, lam_q2, lam_k2)):
            par[l, :, P_LAM + 64 * i:P_LAM + 64 * (i + 1)] = np.asarray(v, f32)[l][None, :]
        par[l, :, P_ROPE] = inv_freq[(p % 16) % 8]
        par[l, :, P_ROPE + 1] = np.where((p % 16) < 8, -1.0, 1.0)
    postg = np.ascontiguousarray(np.asarray(post_norm_g, f32).reshape(DEPTH, 1, D))
    if 'nc' not in _PROG:
        _PROG['nc'] = build_program()
    nc = _PROG['nc']
    in_maps = []
    for c in range(NCORES):
        b = c % BATCH
        in_maps.append({"x": np.ascontiguousarray(x[b]), "pos": np.ascontiguousarray(positions[b:b + 1]), "win": win, "wbr": wbr,
                        "wo": wo, "par": par, "postg": postg})
    res = run_bass_kernel_spmd(nc, in_maps, core_ids=list(range(NCORES)))
    return np.stack([np.asarray(res.results[b]["out"], f32) for b in range(BATCH)], axis=0)
```

```python
import math
import numpy as np
from contextlib import ExitStack
import concourse.bass as bass
import concourse.mybir as mybir
from concourse.bass_utils import run_bass_kernel_spmd

F32 = mybir.dt.float32; BF16 = mybir.dt.bfloat16; I32 = mybir.dt.int32; U8 = mybir.dt.uint8
AF = mybir.ActivationFunctionType; ALU = mybir.AluOpType; AX = mybir.AxisListType

D = 1024; SEQ = 4096; BATCH = 4; DEPTH = 2; NCORES = 8
NT = SEQ // 128; NB = SEQ // 512
EPS = 1e-6
NBLK = 19
FFCOL = NBLK * 512
WCOLS = FFCOL + 8
NPAR = 8 + 24 + 12 + 1 + 1 + 256 + 2
P_G, P_BM, P_CW, P_DG, P_BF, P_LAM, P_ROPE = 0, 8, 32, 44, 45, 46, 302

ENGS = ('pe', 'dve', 'act', 'pool', 'sp')
CAP = 2000; NSLOT = 8; DCAP = 100


class Op:
    __slots__ = ('eng', 'fn', 'deps', 'sig', 'sem', 'val', 'dma', 'nobar')


class Sched:
    def __init__(self):
        self.ops = {e: [] for e in ENGS}
        self.lastw = {}
        self.readers = {}
        self.pending_dma = []

    def op(self, eng, fn, reads=(), writes=(), dma=False, nobar=False, extra=()):
        o = Op(); o.eng = eng; o.fn = fn; o.dma = dma; o.deps = []; o.sig = dma
        o.sem = None; o.val = None; o.nobar = nobar
        deps = set(extra)
        for r in reads:
            w = self.lastw.get(r)
            if w is not None: deps.add(w)
        for r in writes:
            w = self.lastw.get(r)
            if w is not None: deps.add(w)
            rd = self.readers.get(r)
            if rd:
                deps.update(rd[0].values()); deps.update(rd[1])
        for r in writes:
            self.lastw[r] = o; self.readers[r] = ({}, [])
        for r in reads:
            if r in writes: continue
            rd = self.readers.setdefault(r, ({}, []))
            if dma: rd[1].append(o)
            else: rd[0][eng] = o
        for d in deps:
            if d is o: continue
            if d.eng == 'pe' and eng == 'pe' and not d.dma and not dma: continue
            o.deps.append(d)
        self.ops[eng].append(o)
        if dma and not nobar: self.pending_dma.append(o)
        return o

    def barrier(self):
        last = [self.ops[e][-1] for e in ENGS if self.ops[e] and not self.ops[e][-1].dma]
        lastc = []
        for e in ENGS:
            for o in reversed(self.ops[e]):
                if not o.dma and o.fn is not None:
                    lastc.append(o); break
        ex = lastc + self.pending_dma
        self.pending_dma = []
        for e in ENGS:
            self.op(e, None, extra=ex)

    def emit(self, nc):
        for e in ENGS:
            for o in self.ops[e]:
                for d in o.deps: d.sig = True
        with ExitStack() as st:
            semcache = {}

            def getsem(key):
                if key not in semcache:
                    semcache[key] = st.enter_context(nc.semaphore("s_%s" % "_".join(map(str, key))))
                return semcache[key]
            for e in ENGS:
                k = 0; j = 0
                for o in self.ops[e]:
                    if o.fn is None: continue
                    if o.dma:
                        slot = j % NSLOT; n = j // NSLOT
                        o.sem = getsem(('d', e, slot, n // DCAP)); o.val = 16 * (n % DCAP + 1); j += 1
                    elif o.sig:
                        o.sem = getsem(('c', e, k // CAP)); o.val = k % CAP + 1; k += 1
            block = st.enter_context(nc.Block())

            def run(e):
                def body(eng):
                    seen = {}
                    for o in self.ops[e]:
                        need = {}
                        for d in o.deps:
                            if d.sem is None: continue
                            key = id(d.sem)
                            if need.get(key, (None, 0))[1] < d.val: need[key] = (d.sem, d.val)
                        for key, (sem, val) in need.items():
                            if seen.get(key, 0) < val:
                                eng.wait_ge(sem, val); seen[key] = val
                        if o.fn is None: continue
                        ins = o.fn(eng)
                        if o.dma: ins.then_inc(o.sem, 16)
                        elif o.sig: ins.then_inc(o.sem, 1)
                return body
            block.tensor(run('pe')); block.vector(run('dve')); block.scalar(run('act'))
            block.gpsimd(run('pool')); block.sync(run('sp'))


def I(name, *a, **kw):
    return lambda e: getattr(e, name)(*a, **kw)


_DTB = {F32: 4, BF16: 2, I32: 4}


def build_program(debug=False, stop=None):
    nc = bass.Bass("TRN2", target_bir_lowering=False)
    S = Sched()
    x_in = nc.dram_tensor("x", [SEQ, D], F32, kind="ExternalInput").ap()
    pos_in = nc.dram_tensor("pos", [1, SEQ], I32, kind="ExternalInput").ap()
    win = nc.dram_tensor("win", [DEPTH, D, WCOLS], F32, kind="ExternalInput").ap()
    wbr_in = nc.dram_tensor("wbr", [DEPTH, 1536, D], F32, kind="ExternalInput").ap()
    wo_in = nc.dram_tensor("wo", [DEPTH, D, D], F32, kind="ExternalInput").ap()
    par_in = nc.dram_tensor("par", [DEPTH, 128, NPAR], F32, kind="ExternalInput").ap()
    pg_in = nc.dram_tensor("postg", [DEPTH, 1, D], F32, kind="ExternalInput").ap()
    out = nc.dram_tensor("out", [SEQ, D], F32, kind="ExternalOutput").ap()
    sk = "ExternalOutput" if debug else "Internal"
    xs = nc.dram_tensor("xs", [SEQ, D], F32, kind=sk).ap()
    cs_d = nc.dram_tensor("cs_d", [2, 8, 3, SEQ], BF16, kind=sk).ap()
    rot_d = nc.dram_tensor("rot_d", [2, 128, SEQ], BF16, kind=sk).ap()
    rope_d = nc.dram_tensor("rope_d", [2, 128, SEQ], F32, kind=sk).ap()
    yT_d = nc.dram_tensor("yT_d", [12, 128, SEQ], BF16, kind=sk).ap()
    hT_d = nc.dram_tensor("hT_d", [128, 8, SEQ], BF16, kind=sk).ap() if debug else None

    def stop_here(name):
        if stop == name:
            if debug:
                S.op('sp', I('dma_start', out=hT_d, in_=hT[:, :, :]), reads=['hT'], writes=['hT_d'], dma=True)
            S.barrier(); S.emit(nc)
            return True
        return False

    TOTAL = 206 * 1024
    arena = nc.alloc_sbuf_tensor("arena", [128, TOTAL], U8)
    pb = [nc.alloc_psum_tensor("pb%d" % i, [128, 512], F32) for i in range(8)]
    pbn = ['pb%d' % i for i in range(8)]

    class Region:
        def __init__(self, base, size): self.base = base; self.size = size; self.off = 0
        def reset(self): self.off = 0
        def alloc(self, shape, dt):
            n = int(np.prod(shape[1:])) * _DTB[dt]
            n = (n + 31) // 32 * 32
            assert self.off + n <= self.size, ("region overflow", self.off, n, self.size)
            ap = arena[0:shape[0], self.base + self.off: self.base + self.off + int(np.prod(shape[1:])) * _DTB[dt]].bitcast(dt)
            if len(shape) == 3:
                ap = ap.rearrange("p (a b) -> p a b", a=shape[1])
            elif len(shape) == 4:
                ap = ap.rearrange("p (a b c) -> p a b c", a=shape[1], b=shape[2])
            self.off += n
            return ap

    CONST = Region(0, 6 * 1024)
    HT = Region(6 * 1024, 64 * 1024)
    WORK = Region(70 * 1024, TOTAL - 70 * 1024)

    ident = CONST.alloc([128, 128], BF16)
    ones_bf = CONST.alloc([128, 128], BF16)
    tri = CONST.alloc([128, 128], BF16)
    ones_f = CONST.alloc([128, 128], F32)
    par = [CONST.alloc([128, NPAR], F32) for _ in range(DEPTH)]
    small = CONST.alloc([128, 16], F32)
    hT = HT.alloc([128, 8, SEQ], BF16)

    S.op('pool', I('memset', ident[:, :], 1.0), writes=['ident'])
    S.op('pool', I('affine_select', out=ident[:, :], in_=ident[:, :], pattern=[[-1, 128]], compare_op=ALU.is_equal,
                   fill=0.0, base=0, channel_multiplier=1), reads=['ident'], writes=['ident'])
    S.op('pool', I('memset', ones_bf[:, :], 1.0), writes=['ones_bf'])
    S.op('pool', I('memset', ones_f[:, :], 1.0), writes=['ones_f'])
    S.op('pool', I('memset', tri[:, :], 1.0), writes=['tri'])
    S.op('pool', I('affine_select', out=tri[:, :], in_=tri[:, :], pattern=[[1, 128]], compare_op=ALU.is_ge,
                   fill=0.0, base=0, channel_multiplier=-1), reads=['tri'], writes=['tri'])
    for l in range(DEPTH):
        S.op('sp', I('dma_start', out=par[l][:, :], in_=par_in[l]), writes=['par%d' % l], dma=True)

    WORK.reset()
    posi = WORK.alloc([128, SEQ], I32)
    ang = WORK.alloc([128, SEQ], F32)
    tmp = WORK.alloc([128, SEQ], F32)
    r2 = WORK.alloc([128, SEQ], F32)
    TWO_PI = float(2 * np.pi); PI = float(np.pi)
    S.op('sp', I('dma_start', out=posi[:, :], in_=pos_in.partition_broadcast(128)), writes=['posi'], dma=True)
    S.op('dve', I('tensor_copy', out=ang[:, :], in_=posi[:, :]), reads=['posi'], writes=['ang'])
    S.op('dve', I('tensor_scalar', out=ang[:, :], in0=ang[:, :], scalar1=par[0][:, P_ROPE:P_ROPE + 1], scalar2=None, op0=ALU.mult),
         reads=['ang', 'par0'], writes=['ang'])
    S.op('dve', I('tensor_scalar', out=tmp[:, :], in0=ang[:, :], scalar1=float(1.0 / TWO_PI), scalar2=None, op0=ALU.mult),
         reads=['ang'], writes=['tmp'])
    S.op('dve', I('tensor_copy', out=posi[:, :], in_=tmp[:, :]), reads=['tmp'], writes=['posi'])
    S.op('dve', I('tensor_copy', out=tmp[:, :], in_=posi[:, :]), reads=['posi'], writes=['tmp'])
    S.op('dve', I('scalar_tensor_tensor', out=ang[:, :], in0=tmp[:, :], scalar=-TWO_PI, in1=ang[:, :], op0=ALU.mult, op1=ALU.add),
         reads=['tmp', 'ang'], writes=['ang'])
    S.op('dve', I('tensor_scalar', out=ang[:, :], in0=ang[:, :], scalar1=PI, scalar2=-PI, op0=ALU.min, op1=ALU.max),
         reads=['ang'], writes=['ang'])
    S.op('dve', I('tensor_scalar', out=r2[:, :], in0=ang[:, :], scalar1=float(PI / 2), scalar2=None, op0=ALU.add), reads=['ang'], writes=['r2'])
    S.op('dve', I('tensor_single_scalar', out=tmp[:, :], in_=r2[:, :], scalar=PI, op=ALU.is_gt), reads=['r2'], writes=['tmp'])
    S.op('dve', I('scalar_tensor_tensor', out=r2[:, :], in0=tmp[:, :], scalar=-TWO_PI, in1=r2[:, :], op0=ALU.mult, op1=ALU.add),
         reads=['tmp', 'r2'], writes=['r2'])
    S.op('dve', I('tensor_scalar', out=r2[:, :], in0=r2[:, :], scalar1=PI, scalar2=-PI, op0=ALU.min, op1=ALU.max), reads=['r2'], writes=['r2'])
    S.op('act', I('activation', out=tmp[:, :], in_=r2[:, :], func=AF.Sin), reads=['r2'], writes=['tmp'])
    S.op('sp', I('dma_start', out=rope_d[0], in_=tmp[:, :]), reads=['tmp'], writes=['rope_d'], dma=True)
    S.op('act', I('activation', out=r2[:, :], in_=ang[:, :], func=AF.Sin, scale=par[0][:, P_ROPE + 1:P_ROPE + 2]), reads=['ang', 'par0', 'r2'], writes=['r2'])
    S.op('sp', I('dma_start', out=rope_d[1], in_=r2[:, :]), reads=['r2'], writes=['rope_d'], dma=True)
    S.barrier()
    if stop_here('rope'): return nc

    wcount = [0]

    def load_wblock(wbuf, wname, l, col0, ncols=512):
        S.op('pool', I('dma_start', out=wbuf[:, :, 0:ncols], in_=win[l][:, col0:col0 + ncols].rearrange("(c p) n -> p c n", p=128)),
             writes=[wname], dma=True)

    def proj_fm(wbuf, wname, c0, M, nb, bank):
        for kc in range(8):
            S.op('pe', I('matmul', pb[bank][0:M, :], lhsT=wbuf[:, kc, c0:c0 + M], rhs=hT[:, kc, nb * 512:(nb + 1) * 512],
                         start=(kc == 0), stop=(kc == 7)), reads=[wname, 'hT'], writes=[pbn[bank]])

    def proj_tm(wbuf, wname, c0, dst_fn, dname):
        for g in range(NT // 4):
            bank = g % 4
            for i in range(4):
                tt = g * 4 + i
                for kc in range(8):
                    S.op('pe', I('matmul', pb[bank][:, i * 128:(i + 1) * 128], lhsT=hT[:, kc, tt * 128:(tt + 1) * 128],
                                 rhs=wbuf[:, kc, c0:c0 + 128], start=(kc == 0), stop=(kc == 7)), reads=[wname, 'hT'], writes=[pbn[bank]])
            S.op('dve', dst_fn(g, pb[bank]), reads=[pbn[bank]], writes=[dname])

    for l in range(DEPTH):
        lam_init = 0.8 - 0.6 * math.exp(-0.3 * l)
        P = par[l]; pn = 'par%d' % l
        xsrc = x_in if l == 0 else xs
        xdst = xs if l < DEPTH - 1 else out

        WORK.reset()
        xt = [WORK.alloc([128, D], F32) for _ in range(2)]
        xn = [WORK.alloc([128, D], BF16) for _ in range(2)]
        junk = WORK.alloc([128, D], BF16)
        ss = WORK.alloc([128, NT], F32)
        rs = WORK.alloc([128, NT], F32)
        S.op('dve', I('memset', ss[:, :], 0.0), writes=['ss'])
        for t in range(NT):
            b = t % 2
            S.op('sp', I('dma_start', out=xt[b][:, :], in_=xsrc[t * 128:(t + 1) * 128, :]), writes=['xt%d' % b], dma=True)
            S.op('act', I('activation', out=junk[:, :], in_=xt[b][:, :], func=AF.Square, accum_out=ss[:, t:t + 1]),
                 reads=['xt%d' % b, 'ss'], writes=['junk', 'ssc%d' % t])
            S.op('act', I('activation', out=rs[:, t:t + 1], in_=ss[:, t:t + 1], func=AF.Sqrt, bias=EPS, scale=1.0 / D),
                 reads=['ssc%d' % t], writes=['rs%d' % t])
            S.op('dve', I('reciprocal', out=rs[:, t:t + 1], in_=rs[:, t:t + 1]), reads=['rs%d' % t], writes=['rs%d' % t])
            S.op('act', I('mul', out=xn[b][:, :], in_=xt[b][:, :], mul=rs[:, t:t + 1]), reads=['xt%d' % b, 'rs%d' % t], writes=['xn%d' % b])
            pbt = pb[b][:, :].bitcast(BF16)
            for c in range(8):
                S.op('pe', I('transpose', pbt[:, c * 128:(c + 1) * 128], xn[b][:, c * 128:(c + 1) * 128], ident[:, :]),
                     reads=['xn%d' % b, 'ident'], writes=[pbn[b]])
            S.op('dve', I('tensor_tensor', out=hT[:, :, t * 128:(t + 1) * 128], in0=pbt.rearrange("p (c n) -> p c n", c=8),
                          in1=P[:, P_G:P_G + 8].unsqueeze(2).to_broadcast([128, 8, 128]), op=ALU.mult),
                 reads=[pbn[b], pn], writes=['hT'])
        S.barrier()
        if l == 0 and stop_here('A'): return nc

        WORK.reset()
        lt = WORK.alloc([128, 128], F32)
        S.op('dve', I('tensor_tensor', out=lt[:, 0:64], in0=P[:, P_LAM:P_LAM + 64], in1=P[:, P_LAM + 64:P_LAM + 128], op=ALU.mult), reads=[pn], writes=['lt'])
        S.op('dve', I('tensor_tensor', out=lt[:, 64:128], in0=P[:, P_LAM + 128:P_LAM + 192], in1=P[:, P_LAM + 192:P_LAM + 256], op=ALU.mult), reads=[pn, 'lt'], writes=['lt'])
        S.op('dve', I('tensor_reduce', out=small[:, 3:5], in_=lt[:, :].rearrange("p (a b) -> p a b", a=2), axis=AX.X, op=ALU.add), reads=['lt'], writes=['sm34'])
        S.op('act', I('activation', out=small[:, 5:7], in_=small[:, 3:5], func=AF.Exp), reads=['sm34'], writes=['sm56'])
        S.op('dve', I('tensor_tensor', out=small[:, 0:1], in0=small[:, 6:7], in1=small[:, 5:6], op=ALU.subtract), reads=['sm56'], writes=['neglam'])
        S.op('dve', I('tensor_scalar', out=small[:, 0:1], in0=small[:, 0:1], scalar1=float(-lam_init), scalar2=None, op0=ALU.add), reads=['neglam'], writes=['neglam'])
        S.op('dve', I('tensor_scalar', out=small[:, 1:2], in0=P[:, P_BF:P_BF + 1], scalar1=-1.0, scalar2=None, op0=ALU.mult), reads=[pn], writes=['negb'])
        S.op('dve', I('tensor_scalar', out=small[:, 2:3], in0=P[:, P_DG:P_DG + 1], scalar1=float(1.0 - lam_init), scalar2=None, op0=ALU.mult), reads=[pn], writes=['gprime'])

        wff = WORK.alloc([128, 8, 8], BF16)
        et = [WORK.alloc([8, 512], F32) for _ in range(2)]
        Lraw = WORK.alloc([8, SEQ], F32)
        Lc = WORK.alloc([8, SEQ], F32)
        csb = WORK.alloc([8, 3, SEQ], BF16)
        csn = WORK.alloc([8, 3, SEQ], BF16)
        S.op('pool', I('dma_start', out=wff[:, :, :], in_=win[l][:, FFCOL:FFCOL + 8].rearrange("(c p) n -> p c n", p=128)), writes=['wff'], dma=True)
        for nb in range(NB):
            b = nb % 2
            for kc in range(8):
                S.op('pe', I('matmul', pb[b][0:8, :], lhsT=wff[:, kc, 0:8], rhs=hT[:, kc, nb * 512:(nb + 1) * 512], start=(kc == 0), stop=(kc == 7)),
                     reads=['wff', 'hT'], writes=[pbn[b]])
            S.op('act', I('activation', out=et[b][:, :], in_=pb[b][0:8, :], func=AF.Exp, bias=small[0:8, 1:2], scale=-1.0),
                 reads=[pbn[b], 'negb'], writes=['et%d' % b])
            S.op('act', I('activation', out=Lraw[:, nb * 512:(nb + 1) * 512], in_=et[b][:, :], func=AF.Ln, bias=1.0, scale=1.0),
                 reads=['et%d' % b], writes=['Lraw'])
        S.op('dve', I('tensor_tensor_scan', out=Lc[:, :], data0=ones_f[0:8, 0:1].to_broadcast([8, SEQ]), data1=Lraw[:, :], initial=0.0,
                      op0=ALU.mult, op1=ALU.add), reads=['Lraw', 'ones_f'], writes=['Lc'])
        S.op('dve', I('tensor_copy', out=csb[:, 0, :], in_=Lc[:, :]), reads=['Lc'], writes=['csb'])
        S.op('dve', I('tensor_tensor', out=Lraw[:, :], in0=Lc[:, :], in1=csb[:, 0, :], op=ALU.subtract), reads=['Lc', 'csb'], writes=['Lraw'])
        S.op('dve', I('tensor_copy', out=csb[:, 1, :], in_=Lraw[:, :]), reads=['Lraw'], writes=['csb'])
        S.op('dve', I('tensor_tensor', out=Lc[:, :], in0=Lraw[:, :], in1=csb[:, 1, :], op=ALU.subtract), reads=['Lraw', 'csb'], writes=['Lc'])
        S.op('dve', I('tensor_copy', out=csb[:, 2, :], in_=Lc[:, :]), reads=['Lc'], writes=['csb'])
        S.op('dve', I('tensor_scalar', out=csn[:, :, :], in0=csb[:, :, :], scalar1=-1.0, scalar2=None, op0=ALU.mult), reads=['csb'], writes=['csn'])
        S.op('sp', I('dma_start', out=cs_d[0], in_=csb[:, :, :]), reads=['csb'], writes=['cs_d'], dma=True)
        S.op('sp', I('dma_start', out=cs_d[1], in_=csn[:, :, :]), reads=['csn'], writes=['cs_d'], dma=True)
        S.barrier()
        if l == 0 and stop_here('B0'): return nc

        WORK.reset()
        wb = [WORK.alloc([128, 8, 512], BF16) for _ in range(2)]
        QK = [WORK.alloc([128, SEQ], BF16) for _ in range(4)]
        Vaug = WORK.alloc([128, NT, 2, 128], BF16)
        Gb = WORK.alloc([128, SEQ], BF16)
        ybuf = WORK.alloc([128, SEQ], BF16)
        Et = [WORK.alloc([128, 512], BF16) for _ in range(4)]
        rsb = WORK.alloc([128, 512], F32)
        tb = WORK.alloc([128, 512], F32)
        for i in range(4):
            S.op('pool', I('memset', QK[i][64:70, :], 1.0), writes=['QK%d' % i])
        S.op('pool', I('memset', Vaug[:, :, :, 64:128], 1.0), writes=['Vaug'])
        load_wblock(wb[0], 'wb0', l, 0)
        ecount = [0]
        for j in range(4):
            w = wb[j % 2]; wn = 'wb%d' % (j % 2)
            if j + 1 < 4:
                load_wblock(wb[(j + 1) % 2], 'wb%d' % ((j + 1) % 2), l, (j + 1) * 512)
            for nb in range(NB):
                sl = slice(nb * 512, (nb + 1) * 512)
                bq = (2 * nb) % 4; bk = (2 * nb + 1) % 4
                proj_fm(w, wn, 0, 128, nb, bq)
                S.op('act', I('mul', out=QK[0][0:64, sl], in_=pb[bq][0:64, :], mul=0.125), reads=[pbn[bq]], writes=['QK0'])
                S.op('act', I('mul', out=QK[1][0:64, sl], in_=pb[bq][64:128, :], mul=0.125), reads=[pbn[bq]], writes=['QK1'])
                proj_fm(w, wn, 128, 128, nb, bk)
                S.op('dve', I('tensor_copy', out=QK[2][0:64, sl], in_=pb[bk][0:64, :]), reads=[pbn[bk]], writes=['QK2'])
                S.op('dve', I('tensor_copy', out=QK[3][0:64, sl], in_=pb[bk][64:128, :]), reads=[pbn[bk]], writes=['QK3'])
            for hh in range(2):
                head = 2 * j + hh
                S.op('sp', I('dma_start', out=QK[hh][64:67, :], in_=cs_d[1, head]), reads=['cs_d'], writes=['QK%d' % hh], dma=True)
                S.op('sp', I('dma_start', out=QK[2 + hh][67:70, :], in_=cs_d[0, head]), reads=['cs_d'], writes=['QK%d' % (2 + hh)], dma=True)
            for nb in range(NB):
                sl = slice(nb * 512, (nb + 1) * 512)
                bg = nb % 4
                proj_fm(w, wn, 384, 128, nb, bg)
                S.op('act', I('activation', out=Gb[:, sl], in_=pb[bg][:, :], func=AF.Silu), reads=[pbn[bg]], writes=['Gb'])
            proj_tm(w, wn, 256,
                    lambda g, bank: I('tensor_copy', out=Vaug[:, g * 4:(g + 1) * 4, :, 0:64],
                                      in_=bank[:, :].rearrange("p (t h d) -> p t h d", t=4, h=2)), 'Vaug')
            for hh in range(2):
                Qn, Kn = 'QK%d' % hh, 'QK%d' % (2 + hh)
                Qa, Ka = QK[hh], QK[2 + hh]
                r0 = 64 * hh
                T = [(qb, kt) for qb in range(NB) for kt in range(4 * (qb + 1))]
                info = {}
                SBK = [3, 4, 5]; LOOK = 2

                def emit_S(i):
                    qb, kt = T[i]
                    jd = kt - 4 * qb
                    c0 = 128 * jd if jd > 0 else 0
                    sb = SBK[ecount[0] % 3]; E = Et[ecount[0] % 4]; En = 'E%d' % (ecount[0] % 4); ecount[0] += 1
                    info[i] = (sb, E, En, c0, jd)
                    S.op('pe', I('matmul', pb[sb][:, c0:512], lhsT=Ka[0:70, kt * 128:(kt + 1) * 128], rhs=Qa[0:70, qb * 512 + c0:(qb + 1) * 512],
                                 start=True, stop=True), reads=[Qn, Kn], writes=[pbn[sb]])

                def emit_rest(i):
                    qb, kt = T[i]
                    sb, E, En, c0, jd = info.pop(i)
                    nkt = 4 * (qb + 1); ob = 6 + (qb % 2)
                    S.op('act', I('activation', out=E[:, c0:512], in_=pb[sb][:, c0:512], func=AF.Exp), reads=[pbn[sb]], writes=[En])
                    if jd >= 0:
                        S.op('dve', I('tensor_tensor', out=E[:, c0:c0 + 128], in0=E[:, c0:c0 + 128], in1=tri[:, :], op=ALU.mult), reads=[En, 'tri'], writes=[En])
                    S.op('pe', I('matmul', pb[ob][:, c0:512], lhsT=Vaug[:, kt, hh, :], rhs=E[:, c0:512], start=(kt == 0), stop=(kt == nkt - 1),
                                 skip_group_check=True), reads=[En, 'Vaug'], writes=[pbn[ob]])
                    if kt == nkt - 1:
                        sl = slice(qb * 512, (qb + 1) * 512)
                        S.op('dve', I('reciprocal', out=rsb[64:128, :], in_=pb[ob][64:128, :]), reads=[pbn[ob]], writes=['rsb'])
                        S.op('dve', I('tensor_tensor', out=tb[r0:r0 + 64, :], in0=pb[ob][0:64, :], in1=rsb[64:128, :], op=ALU.mult), reads=[pbn[ob], 'rsb'], writes=['tb'])
                        S.op('dve', I('tensor_tensor', out=ybuf[r0:r0 + 64, sl], in0=tb[r0:r0 + 64, :], in1=Gb[r0:r0 + 64, sl], op=ALU.mult), reads=['tb', 'Gb'], writes=['ybuf'])
                for i in range(len(T) + LOOK):
                    if i < len(T): emit_S(i)
                    if i >= LOOK: emit_rest(i - LOOK)
            S.op('sp', I('dma_start', out=yT_d[j], in_=ybuf[:, :]), reads=['ybuf'], writes=['yT_d'], dma=True)
        S.barrier()
        if l == 0 and stop_here('B1'): return nc

        WORK.reset()
        wb = [WORK.alloc([128, 8, 512], BF16) for _ in range(2)]
        rot = [WORK.alloc([128, SEQ], BF16) for _ in range(2)]
        ctab = [WORK.alloc([128, 512], F32) for _ in range(2)]
        stab = [WORK.alloc([128, 512], F32) for _ in range(2)]
        t1 = WORK.alloc([128, 512], F32)
        t2 = WORK.alloc([128, 512], F32)
        load_wblock(wb[0], 'wb0', l, 4 * 512)
        load_wblock(wb[1], 'wb1', l, 5 * 512)
        for nb in range(NB):
            sl = slice(nb * 512, (nb + 1) * 512)
            b = nb % 2
            S.op('sp', I('dma_start', out=ctab[b][:, :], in_=rope_d[0][:, sl]), reads=['rope_d'], writes=['ctab%d' % b], dma=True)
            S.op('sp', I('dma_start', out=stab[b][:, :], in_=rope_d[1][:, sl]), reads=['rope_d'], writes=['stab%d' % b], dma=True)
            for qk in range(2):
                ba = (2 * qk) % 4; bb = (2 * qk + 1) % 4
                proj_fm(wb[0], 'wb0', qk * 256, 128, nb, ba)
                proj_fm(wb[0], 'wb0', qk * 256 + 128, 128, nb, bb)
                S.op('dve', I('tensor_tensor', out=t1[:, :], in0=pb[ba][:, :], in1=ctab[b][:, :], op=ALU.mult), reads=[pbn[ba], 'ctab%d' % b], writes=['t1'])
                S.op('dve', I('tensor_tensor', out=t2[:, :], in0=pb[bb][:, :], in1=stab[b][:, :], op=ALU.mult), reads=[pbn[bb], 'stab%d' % b], writes=['t2'])
                S.op('dve', I('tensor_tensor', out=rot[qk][:, sl], in0=t1[:, :], in1=t2[:, :], op=ALU.add), reads=['t1', 't2'], writes=['rot%d' % qk])
        for qk in range(2):
            S.op('sp', I('dma_start', out=rot_d[qk], in_=rot[qk][:, :]), reads=['rot%d' % qk], writes=['rot_d'], dma=True)
        S.barrier()
        if l == 0 and stop_here('B2'): return nc

        WORK.reset()
        wb = [WORK.alloc([128, 8, 512], BF16) for _ in range(2)]
        Qc = [WORK.alloc([128, SEQ], BF16) for _ in range(2)]
        Kc = [WORK.alloc([128, SEQ], BF16) for _ in range(2)]
        Vd = WORK.alloc([128, NT, 128], BF16)
        Gd = WORK.alloc([128, SEQ], BF16)
        ybuf = WORK.alloc([128, SEQ], BF16)
        Et = [WORK.alloc([128, 512], BF16) for _ in range(4)]
        rzb = WORK.alloc([128, 512], F32)
        tnb = [WORK.alloc([128, 512], F32) for _ in range(2)]
        obf = [WORK.alloc([128, 512], F32) for _ in range(2)]
        sqb = [WORK.alloc([128, 512], F32) for _ in range(2)]
        lnb = [WORK.alloc([128, 512], F32) for _ in range(2)]
        zacc = [[[WORK.alloc([128, 512], F32) for _ in range(2)] for _ in range(2)] for _ in range(2)]
        ecount = [0]
        for h in range(4):
            w = wb[(h + 1) % 2]; wn = 'wb%d' % ((h + 1) % 2)
            if h + 1 < 4:
                load_wblock(wb[h % 2], 'wb%d' % (h % 2), l, (6 + h) * 512)
            for nb in range(NB):
                sl = slice(nb * 512, (nb + 1) * 512)
                bq = (3 * nb) % 4; bk = (3 * nb + 1) % 4; bg = (3 * nb + 2) % 4
                proj_fm(w, wn, 0, 128, nb, bq)
                for c in range(2):
                    S.op('act', I('copy', out=Qc[c][0:64, sl], in_=pb[bq][64 * c:64 * c + 64, :]), reads=[pbn[bq]], writes=['Qc%d' % c])
                proj_fm(w, wn, 128, 128, nb, bk)
                for c in range(2):
                    S.op('dve', I('tensor_copy', out=Kc[c][0:64, sl], in_=pb[bk][64 * c:64 * c + 64, :]), reads=[pbn[bk]], writes=['Kc%d' % c])
                proj_fm(w, wn, 384, 128, nb, bg)
                S.op('act', I('activation', out=Gd[:, sl], in_=pb[bg][:, :], func=AF.Silu), reads=[pbn[bg]], writes=['Gd'])
            for c in range(2):
                g = 2 * h + c
                S.op('sp', I('dma_start', out=Qc[c][0:16, :], in_=rot_d[0][16 * g:16 * g + 16, :]), reads=['rot_d'], writes=['Qc%d' % c], dma=True)
                S.op('sp', I('dma_start', out=Kc[c][0:16, :], in_=rot_d[1][16 * g:16 * g + 16, :]), reads=['rot_d'], writes=['Kc%d' % c], dma=True)
            proj_tm(w, wn, 256,
                    lambda g, bank: I('tensor_copy', out=Vd[:, g * 4:(g + 1) * 4, :], in_=bank[:, :].rearrange("p (t d) -> p t d", t=4)), 'Vd')
            T = [(qb, c, kt) for qb in range(NB) for c in range(2) for kt in range(4 * (qb + 1))]
            info = {}
            SBK = [1, 4, 5]; OBK = [[6, 7], [2, 3]]; LOOK = 2; LAG = 3
            pending = []

            def emit_S(i):
                qb, c, kt = T[i]
                jd = kt - 4 * qb
                c0 = 128 * jd if jd > 0 else 0
                sb = SBK[ecount[0] % 3]; E = Et[ecount[0] % 4]; En = 'E%d' % (ecount[0] % 4); ecount[0] += 1
                info[i] = (sb, E, En, c0, jd)
                S.op('pe', I('matmul', pb[sb][:, c0:512], lhsT=Kc[c][0:64, kt * 128:(kt + 1) * 128],
                             rhs=Qc[c][0:64, qb * 512 + c0:(qb + 1) * 512], start=True, stop=True), reads=['Qc%d' % c, 'Kc%d' % c], writes=[pbn[sb]])

            def part2(qb):
                par2 = qb % 2; sl = slice(qb * 512, (qb + 1) * 512)
                ob_, sq_, ln_ = obf[par2], sqb[par2], lnb[par2]
                on_, sn_, lnn_ = 'obf%d' % par2, 'sqb%d' % par2, 'lnb%d' % par2
                S.op('pe', I('matmul', pb[0][:, :], lhsT=ones_f[:, :], rhs=sq_[:, :], start=True, stop=True), reads=[sn_, 'ones_f'], writes=[pbn[0]])
                S.op('act', I('activation', out=ln_[:, :], in_=pb[0][:, :], func=AF.Ln, bias=EPS, scale=1.0 / 128), reads=[pbn[0]], writes=[lnn_])
                S.op('act', I('activation', out=ln_[:, :], in_=ln_[:, :], func=AF.Exp, scale=-0.5), reads=[lnn_], writes=[lnn_])
                S.op('dve', I('tensor_tensor', out=ob_[:, :], in0=ob_[:, :], in1=ln_[:, :], op=ALU.mult), reads=[on_, lnn_], writes=[on_])
                S.op('dve', I('scalar_tensor_tensor', out=ybuf[:, sl], in0=ob_[:, :], scalar=small[:, 2:3], in1=Gd[:, sl], op0=ALU.mult, op1=ALU.mult),
                     reads=[on_, 'gprime', 'Gd'], writes=['ybuf'])

            def emit_rest(i):
                qb, c, kt = T[i]
                sb, E, En, c0, jd = info.pop(i)
                nkt = 4 * (qb + 1); ob = OBK[qb % 2][c]
                S.op('act', I('activation', out=E[:, c0:512], in_=pb[sb][:, c0:512], func=AF.Exp, scale=0.125), reads=[pbn[sb]], writes=[En])
                if jd >= 0:
                    S.op('dve', I('tensor_tensor', out=E[:, c0:c0 + 128], in0=E[:, c0:c0 + 128], in1=tri[:, :], op=ALU.mult), reads=[En, 'tri'], writes=[En])
                S.op('pe', I('matmul', pb[ob][:, c0:512], lhsT=Vd[:, kt, :], rhs=E[:, c0:512], start=(kt == 0), stop=(kt == nkt - 1),
                             skip_group_check=True), reads=[En, 'Vd'], writes=[pbn[ob]])
                ae = 'dve'
                za = zacc[qb % 2][c][kt % 2]; zn = 'zacc%d%d%d' % (qb % 2, c, kt % 2)
                if kt < 2:
                    if c0 > 0:
                        S.op(ae, I('memset', za[:, 0:c0], 0.0), writes=[zn])
                    S.op(ae, I('tensor_copy', out=za[:, c0:512], in_=E[:, c0:512]), reads=[En], writes=[zn])
                else:
                    S.op(ae, I('tensor_tensor', out=za[:, c0:512], in0=za[:, c0:512], in1=E[:, c0:512], op=ALU.add), reads=[En, zn], writes=[zn])
                if kt == nkt - 1:
                    par2 = qb % 2
                    for hz in range(2):
                        S.op('pe', I('matmul', pb[0][:, :], lhsT=ones_f[:, :], rhs=zacc[par2][c][hz][:, :], start=(hz == 0), stop=(hz == 1)),
                             reads=['zacc%d%d%d' % (par2, c, hz), 'ones_f'], writes=[pbn[0]])
                    S.op('dve', I('reciprocal', out=rzb[:, :], in_=pb[0][:, :]), reads=[pbn[0]], writes=['rzb'])
                    S.op('dve', I('tensor_tensor', out=tnb[c][:, :], in0=pb[ob][:, :], in1=rzb[:, :], op=ALU.mult), reads=[pbn[ob], 'rzb'], writes=['tnb%d' % c])
                    if c == 1:
                        ob_, sq_ = obf[par2], sqb[par2]
                        on_, sn_ = 'obf%d' % par2, 'sqb%d' % par2
                        S.op('dve', I('scalar_tensor_tensor', out=ob_[:, :], in0=tnb[1][:, :], scalar=small[:, 0:1], in1=tnb[0][:, :],
                                      op0=ALU.mult, op1=ALU.add), reads=['tnb0', 'tnb1', 'neglam'], writes=[on_])
                        S.op('dve', I('tensor_tensor', out=sq_[:, :], in0=ob_[:, :], in1=ob_[:, :], op=ALU.mult), reads=[on_], writes=[sn_])
                        pending.append((i + LAG, qb))
                while pending and pending[0][0] <= i:
                    part2(pending.pop(0)[1])
            for i in range(len(T) + LOOK):
                if i < len(T): emit_S(i)
                if i >= LOOK: emit_rest(i - LOOK)
            while pending:
                part2(pending.pop(0)[1])
            S.op('sp', I('dma_start', out=yT_d[4 + h], in_=ybuf[:, :]), reads=['ybuf'], writes=['yT_d'], dma=True)
        S.barrier()
        if l == 0 and stop_here('B3'): return nc

        WORK.reset()
        wb = [WORK.alloc([128, 8, 512], BF16) for _ in range(2)]
        ubuf = WORK.alloc([128, SEQ + 2], F32)
        ybuf = WORK.alloc([128, SEQ], BF16)
        cxs = WORK.alloc([128, 512], F32)
        acc = WORK.alloc([128, 512], F32)
        sg = WORK.alloc([128, 512], F32)
        S.op('dve', I('memset', ubuf[:, 0:2], 0.0), writes=['ubuf'])
        load_wblock(wb[0], 'wb0', l, 9 * 512)
        for j in range(4):
            w = wb[j % 2]; wn = 'wb%d' % (j % 2)
            if j + 1 < 4:
                load_wblock(wb[(j + 1) % 2], 'wb%d' % ((j + 1) % 2), l, (10 + j) * 512)
            cw = lambda k: P[:, P_CW + 3 * j + k:P_CW + 3 * j + k + 1]
            for nb in range(NB):
                sl = slice(nb * 512, (nb + 1) * 512)
                o4 = 4 * (nb % 2)
                bcb, bcc, bcx, bcg = o4, o4 + 1, o4 + 2, o4 + 3
                proj_fm(w, wn, 256, 128, nb, bcx)
                S.op('act', I('copy', out=cxs[:, :], in_=pb[bcx][:, :]), reads=[pbn[bcx]], writes=['cxs'])
                proj_fm(w, wn, 128, 128, nb, bcc)
                S.op('dve', I('tensor_tensor', out=ubuf[:, 2 + nb * 512:2 + (nb + 1) * 512], in0=pb[bcc][:, :], in1=cxs[:, :], op=ALU.mult),
                     reads=[pbn[bcc], 'cxs'], writes=['ubuf'])
                proj_fm(w, wn, 384, 128, nb, bcg)
                S.op('act', I('activation', out=sg[:, :], in_=pb[bcg][:, :], func=AF.Silu), reads=[pbn[bcg]], writes=['sg'])
                proj_fm(w, wn, 0, 128, nb, bcb)
                S.op('dve', I('tensor_scalar', out=acc[:, :], in0=ubuf[:, 2 + nb * 512:2 + (nb + 1) * 512], scalar1=cw(2), scalar2=None, op0=ALU.mult),
                     reads=['ubuf', pn], writes=['acc'])
                S.op('dve', I('scalar_tensor_tensor', out=acc[:, :], in0=ubuf[:, 1 + nb * 512:1 + (nb + 1) * 512], scalar=cw(1), in1=acc[:, :],
                              op0=ALU.mult, op1=ALU.add), reads=['ubuf', 'acc', pn], writes=['acc'])
                S.op('dve', I('scalar_tensor_tensor', out=acc[:, :], in0=ubuf[:, nb * 512:(nb + 1) * 512], scalar=cw(0), in1=acc[:, :],
                              op0=ALU.mult, op1=ALU.add), reads=['ubuf', 'acc', pn], writes=['acc'])
                S.op('dve', I('tensor_tensor', out=acc[:, :], in0=pb[bcb][:, :], in1=acc[:, :], op=ALU.mult), reads=[pbn[bcb], 'acc'], writes=['acc'])
                S.op('dve', I('tensor_tensor', out=ybuf[:, sl], in0=acc[:, :], in1=sg[:, :], op=ALU.mult), reads=['acc', 'sg'], writes=['ybuf'])
            S.op('sp', I('dma_start', out=yT_d[8 + j], in_=ybuf[:, :]), reads=['ybuf'], writes=['yT_d'], dma=True)
        S.barrier()
        if l == 0 and stop_here('B4'): return nc

        WORK.reset()
        wmg = WORK.alloc([128, 8, 3072], BF16)
        wbr = WORK.alloc([128, 12, D], BF16)
        wo = WORK.alloc([128, 8, D], BF16)
        ybk = WORK.alloc([128, 12, 512], BF16)
        mT = WORK.alloc([128, 8, 512], BF16)
        pgb = WORK.alloc([128, D], F32)
        xres = WORK.alloc([128, D], F32)
        xnew = WORK.alloc([128, D], F32)
        sgt = [WORK.alloc([128, 512], F32) for _ in range(3)]
        tt1 = WORK.alloc([128, 512], F32)
        tt2 = WORK.alloc([128, 512], F32)
        sso = WORK.alloc([128, 4], F32)
        for i in range(6):
            S.op('pool', I('dma_start', out=wmg[:, :, i * 512:(i + 1) * 512], in_=win[l][:, (13 + i) * 512:(14 + i) * 512].rearrange("(c p) n -> p c n", p=128)),
                 writes=['wmg'], dma=True)
        for i in range(3):
            S.op('pool', I('dma_start', out=wbr[:, 4 * i:4 * i + 4, :], in_=wbr_in[l][512 * i:512 * (i + 1), :].rearrange("(c p) n -> p c n", p=128)),
                 writes=['wbr'], dma=True)
        for i in range(2):
            S.op('pool', I('dma_start', out=wo[:, 4 * i:4 * i + 4, :], in_=wo_in[l][512 * i:512 * (i + 1), :].rearrange("(c p) n -> p c n", p=128)),
                 writes=['wo'], dma=True)
        S.op('sp', I('dma_start', out=pgb[:, :], in_=pg_in[l].partition_broadcast(128)), writes=['pgb'], dma=True)
        for nb in range(NB):
            sl = slice(nb * 512, (nb + 1) * 512)
            S.op('sp', I('dma_start', out=ybk[:, :, :], in_=yT_d[:, :, sl].rearrange("c p n -> p c n")), reads=['yT_d'], writes=['ybk'], dma=True)
            for oc in range(8):
                for br in range(3):
                    for kc in range(4):
                        S.op('pe', I('matmul', pb[br][:, :], lhsT=wbr[:, br * 4 + kc, oc * 128:(oc + 1) * 128], rhs=ybk[:, br * 4 + kc, :],
                                     start=(kc == 0), stop=(kc == 3)), reads=['wbr', 'ybk'], writes=[pbn[br]])
                for br in range(3):
                    c0 = br * 1024 + oc * 128
                    for kc in range(8):
                        S.op('pe', I('matmul', pb[3 + br][:, :], lhsT=wmg[:, kc, c0:c0 + 128], rhs=hT[:, kc, sl],
                                     start=(kc == 0), stop=(kc == 7)), reads=['wmg', 'hT'], writes=[pbn[3 + br]])
                    S.op('act', I('activation', out=sgt[br][:, :], in_=pb[3 + br][:, :], func=AF.Sigmoid, bias=P[:, P_BM + br * 8 + oc:P_BM + br * 8 + oc + 1]),
                         reads=[pbn[3 + br], pn], writes=['sgt%d' % br])
                S.op('dve', I('tensor_tensor', out=tt1[:, :], in0=pb[0][:, :], in1=sgt[0][:, :], op=ALU.mult), reads=[pbn[0], 'sgt0'], writes=['tt1'])
                S.op('dve', I('tensor_tensor', out=tt2[:, :], in0=pb[1][:, :], in1=sgt[1][:, :], op=ALU.mult), reads=[pbn[1], 'sgt1'], writes=['tt2'])
                S.op('dve', I('tensor_tensor', out=tt1[:, :], in0=tt1[:, :], in1=tt2[:, :], op=ALU.add), reads=['tt1', 'tt2'], writes=['tt1'])
                S.op('dve', I('tensor_tensor', out=tt2[:, :], in0=pb[2][:, :], in1=sgt[2][:, :], op=ALU.mult), reads=[pbn[2], 'sgt2'], writes=['tt2'])
                S.op('dve', I('tensor_tensor', out=mT[:, oc, :], in0=tt1[:, :], in1=tt2[:, :], op=ALU.add), reads=['tt1', 'tt2'], writes=['mT'])
            for ti in range(4):
                t = nb * 4 + ti
                S.op('sp', I('dma_start', out=xres[:, :], in_=xsrc[t * 128:(t + 1) * 128, :]), writes=['xres'], dma=True)
                S.op('dve', I('memset', sso[:, 0:2], 0.0), writes=['ssh0', 'ssh1'])
                for hf in range(2):
                    for kc in range(8):
                        S.op('pe', I('matmul', pb[6 + hf][:, :], lhsT=mT[:, kc, ti * 128:(ti + 1) * 128], rhs=wo[:, kc, hf * 512:(hf + 1) * 512],
                                     start=(kc == 0), stop=(kc == 7)), reads=['mT', 'wo'], writes=[pbn[6 + hf]])
                    S.op('act', I('activation', out=sgt[hf][:, :], in_=pb[6 + hf][:, :], func=AF.Square, accum_out=sso[:, hf:hf + 1]),
                         reads=[pbn[6 + hf], 'ssh%d' % hf], writes=['sgt%d' % hf, 'ssh%d' % hf])
                S.op('dve', I('tensor_tensor', out=sso[:, 2:3], in0=sso[:, 0:1], in1=sso[:, 1:2], op=ALU.add), reads=['ssh0', 'ssh1'], writes=['sso2'])
                S.op('act', I('activation', out=sso[:, 3:4], in_=sso[:, 2:3], func=AF.Sqrt, bias=EPS, scale=1.0 / D), reads=['sso2'], writes=['sso3'])
                S.op('dve', I('reciprocal', out=sso[:, 3:4], in_=sso[:, 3:4]), reads=['sso3'], writes=['sso3'])
                for hf in range(2):
                    hs = slice(hf * 512, (hf + 1) * 512)
                    S.op('dve', I('scalar_tensor_tensor', out=xnew[:, hs], in0=pb[6 + hf][:, :], scalar=sso[:, 3:4], in1=pgb[:, hs], op0=ALU.mult, op1=ALU.mult),
                         reads=[pbn[6 + hf], 'sso3', 'pgb'], writes=['xnew%d' % hf])
                    S.op('dve', I('tensor_tensor', out=xnew[:, hs], in0=xnew[:, hs], in1=xres[:, hs], op=ALU.add), reads=['xnew%d' % hf, 'xres'], writes=['xnew%d' % hf])
                S.op('sp', I('dma_start', out=xdst[t * 128:(t + 1) * 128, :], in_=xnew[:, :]), reads=['xnew0', 'xnew1'], writes=['xout%d_%d' % (l, t)], dma=True)
        S.barrier()
        if l == 0 and stop_here('C'): return nc
    S.barrier()
    S.emit(nc)
    return nc


O_FQ, O_FK, O_FV, O_FF, O_FG = 0, 512, 1024, 1536, 1544
O_DQ, O_DK, O_DV, O_DG = 2056, 2568, 3080, 3592
O_CB, O_CC, O_CX, O_CG, O_MG = 4104, 4616, 5128, 5640, 6152


def _col_index():
    idx = []
    r128 = np.arange(128)
    for j in range(4):
        for o in (O_FQ, O_FK, O_FV, O_FG):
            idx.append(o + j * 128 + r128)
    g = np.repeat(np.arange(8), 16); jj = np.tile(np.arange(16), 8)
    jsw = np.where(jj < 8, jj + 8, jj - 8)
    for o in (O_DQ, O_DK):
        idx.append(o + g * 64 + jj)
        idx.append(o + g * 64 + jsw)
    for h in range(4):
        for o in (O_DQ, O_DK, O_DV, O_DG):
            idx.append(o + h * 128 + r128)
    for j in range(4):
        for o in (O_CB, O_CC, O_CX, O_CG):
            idx.append(o + j * 128 + r128)
    idx.append(O_MG + np.arange(3072))
    idx.append(O_FF + np.arange(8))
    idx = np.concatenate(idx)
    assert idx.shape[0] == WCOLS
    return idx


_PROG = {}


def kernel(x, positions, pre_norm_g, w_in, b_forget, b_merge, conv_w, lam_q1, lam_k1, lam_q2, lam_k2,
           diff_norm_g, w_br_fox, w_br_diff, w_br_conv, w_out, post_norm_g):
    f32 = np.float32
    x = np.asarray(x, f32); positions = np.asarray(positions, np.int32)
    idx = _col_index()
    win = np.ascontiguousarray(np.asarray(w_in, f32)[:, :, idx])
    wbr = np.ascontiguousarray(np.concatenate([np.asarray(w_br_fox, f32), np.asarray(w_br_diff, f32), np.asarray(w_br_conv, f32)], axis=1))
    wo = np.ascontiguousarray(np.asarray(w_out, f32))
    par = np.zeros((DEPTH, 128, NPAR), f32)
    p = np.arange(128)
    inv_freq = (500000.0 ** (-np.arange(0, 16, 2, dtype=np.float32) / 16)).astype(f32)
    for l in range(DEPTH):
        par[l, :, P_G:P_G + 8] = np.asarray(pre_norm_g, f32)[l].reshape(8, 128).T
        par[l, :, P_BM:P_BM + 24] = np.asarray(b_merge, f32)[l].reshape(24, 128).T
        cw = np.asarray(conv_w, f32)[l]
        par[l, :, P_CW:P_CW + 12] = cw.reshape(3, 4, 128).transpose(2, 1, 0).reshape(128, 12)
        par[l, :, P_DG] = np.asarray(diff_norm_g, f32)[l]
        par[l, 0:8, P_BF] = np.asarray(b_forget, f32)[l]
        for i, v in enumerate((lam_q1, lam_k1, lam_q2, lam_k2)):
            par[l, :, P_LAM + 64 * i:P_LAM + 64 * (i + 1)] = np.asarray(v, f32)[l][None, :]
        par[l, :, P_ROPE] = inv_freq[(p % 16) % 8]
        par[l, :, P_ROPE + 1] = np.where((p % 16) < 8, -1.0, 1.0)
    postg = np.ascontiguousarray(np.asarray(post_norm_g, f32).reshape(DEPTH, 1, D))
    if 'nc' not in _PROG:
        _PROG['nc'] = build_program()
    nc = _PROG['nc']
    in_maps = []
    for c in range(NCORES):
        b = c % BATCH
        in_maps.append({"x": np.ascontiguousarray(x[b]), "pos": np.ascontiguousarray(positions[b:b + 1]), "win": win, "wbr": wbr,
                        "wo": wo, "par": par, "postg": postg})
    res = run_bass_kernel_spmd(nc, in_maps, core_ids=list(range(NCORES)))
    return np.stack([np.asarray(res.results[b]["out"], f32) for b in range(BATCH)], axis=0)
```
